# Optimizing a Trainium2 kernel written in Bass

```python
import math
import jax, jax.numpy as jnp
from jax import lax
import numpy as np

D_MODEL = 4096
BATCH = 1
SEQ = 8192
DEPTH = 2

HEAD_DIM = 128
A_GROUPS = 16
A_WIDTH = A_GROUPS * HEAD_DIM
CHUNK = 128
B_HEADS = 16
B_WIDTH = B_HEADS * HEAD_DIM
DILATED_CONFIGS = ((128, 1), (512, 4), (2048, 16))
ATTN_BLOCK = 128
ROPE_THETA = 500000.0
ROT_DIM = HEAD_DIM // 4
D_FF = ((8 * D_MODEL // 3 + 255) // 256) * 256
PL_DIM = 256
IN_WIDTH = 2 * A_WIDTH + 3 * B_WIDTH + 2 * D_MODEL
EPS = 1e-6
NEG_INF = -1e30

kernel_name = "hybrid_gmlp_dilated_attn_block"


def rms_norm(x, g):
    xf = x.astype(jnp.float32)
    y = xf * lax.rsqrt(jnp.mean(xf * xf, axis=-1, keepdims=True) + EPS)
    return (y * g.astype(jnp.float32)).astype(x.dtype)


def layer_norm(x, g, b):
    xf = x.astype(jnp.float32)
    mu = jnp.mean(xf, axis=-1, keepdims=True)
    var = jnp.mean(jnp.square(xf - mu), axis=-1, keepdims=True)
    y = (xf - mu) * lax.rsqrt(var + EPS)
    return (y * g.astype(jnp.float32) + b.astype(jnp.float32)).astype(x.dtype)


def partial_rotary(x, positions):
    half = ROT_DIM // 2
    inv_freq = jnp.power(jnp.float32(ROPE_THETA), -jnp.arange(half, dtype=jnp.float32) / half)
    ang = positions.astype(jnp.float32)[..., None] * inv_freq
    cos = jnp.cos(ang)[:, :, None, :]
    sin = jnp.sin(ang)[:, :, None, :]
    xf = x.astype(jnp.float32)
    x1 = xf[..., :half]
    x2 = xf[..., half:ROT_DIM]
    out = jnp.concatenate([x1 * cos - x2 * sin, x2 * cos + x1 * sin, xf[..., ROT_DIM:]], axis=-1)
    return out.astype(x.dtype)


def chunked_spatial_gating(u, v, ln_g, ln_b, w_s, b_s):
    B, S, _ = v.shape
    vn = layer_norm(v, ln_g, ln_b)
    vc = vn.reshape(B, S // CHUNK, CHUNK, A_GROUPS, HEAD_DIM)
    causal = jnp.tril(jnp.ones((CHUNK, CHUNK), dtype=bool))
    w = jnp.where(causal[None], w_s, jnp.zeros((), w_s.dtype))
    s = jnp.einsum('gts,bnsgc->bntgc', w, vc) + b_s.T[None, None, :, :, None]
    return u * s.reshape(B, S, A_WIDTH)


def dilated_branch(q, k, v, dilation, n_steps):
    B, S, H, Dh = q.shape
    L = S // dilation
    Lp = -(-L // ATTN_BLOCK) * ATTN_BLOCK
    nb = Lp // ATTN_BLOCK

    def to_blocks(t):
        t = t.reshape(B, L, dilation, H, Dh).transpose(0, 2, 3, 1, 4)
        t = jnp.pad(t, ((0, 0), (0, 0), (0, 0), (0, Lp - L), (0, 0)))
        return t.reshape(B, dilation, H, nb, ATTN_BLOCK, Dh)

    def with_prev(t):
        prev = jnp.pad(t, ((0, 0), (0, 0), (0, 0), (1, 0), (0, 0), (0, 0)))[:, :, :, :-1]
        return jnp.concatenate([prev, t], axis=4)

    qb = to_blocks(q)
    kb = with_prev(to_blocks(k))
    vb = with_prev(to_blocks(v))
    s = jnp.einsum('bdhnqe,bdhnke->bdhnqk', qb, kb, preferred_element_type=jnp.float32)
    qi = jnp.arange(ATTN_BLOCK)[:, None]
    kj = jnp.arange(2 * ATTN_BLOCK)[None, :]
    delta = qi + ATTN_BLOCK - kj
    band = (delta >= 0) & (delta <= n_steps)
    has_prev = (jnp.arange(nb) > 0)[:, None, None]
    mask = band[None] & (has_prev | (kj >= ATTN_BLOCK)[None])
    s = jnp.where(mask, s, NEG_INF)
    m = jnp.max(s, axis=-1, keepdims=True)
    e = jnp.exp(s - m)
    l = jnp.sum(e, axis=-1, keepdims=True)
    o = jnp.einsum('bdhnqk,bdhnke->bdhnqe', e, vb.astype(jnp.float32)) / l

    def from_blocks(t):
        c = t.shape[-1]
        t = t.reshape(B, dilation, H, Lp, c)[:, :, :, :L]
        return t.transpose(0, 3, 1, 2, 4).reshape(B, S, H, c)

    return from_blocks(o), from_blocks(m)[..., 0], from_blocks(l)[..., 0]


def dilated_attention(q, k, v):
    outs, maxes, dens = [], [], []
    for window, dilation in DILATED_CONFIGS:
        o, m, l = dilated_branch(q, k, v, dilation, window // dilation)
        outs.append(o)
        maxes.append(m)
        dens.append(l)
    o = jnp.stack(outs)
    m = jnp.stack(maxes)
    l = jnp.stack(dens)
    wts = jnp.exp(m - jnp.max(m, axis=0, keepdims=True)) * l
    out = jnp.sum(wts[..., None] * o, axis=0) / jnp.sum(wts, axis=0)[..., None]
    return out.astype(q.dtype)


def setup_inputs(seed: int = 0) -> dict:
    key = jax.random.key(seed)
    ks = jax.random.split(key, 24)
    f32 = jnp.float32

    def nrm(k, shape, fan_in):
        return jax.random.normal(k, shape, f32) * (fan_in ** -0.5)

    def gain(k, shape):
        return 1.0 + 0.05 * jax.random.normal(k, shape, f32)

    return {
        "x": jax.random.normal(ks[0], (BATCH, SEQ, D_MODEL), f32),
        "p": jax.random.normal(ks[1], (DEPTH, BATCH, SEQ, PL_DIM), f32),
        "positions": jnp.broadcast_to(jnp.arange(SEQ, dtype=jnp.int32)[None, :], (BATCH, SEQ)),
        "norm_mix": gain(ks[2], (DEPTH, D_MODEL)),
        "w_in": nrm(ks[3], (DEPTH, D_MODEL, IN_WIDTH), D_MODEL),
        "v_ln_g": gain(ks[4], (DEPTH, A_WIDTH)),
        "v_ln_b": 0.01 * jax.random.normal(ks[5], (DEPTH, A_WIDTH), f32),
        "w_spatial": nrm(ks[6], (DEPTH, A_GROUPS, CHUNK, CHUNK), CHUNK),
        "b_spatial": 1.0 + 0.05 * jax.random.normal(ks[7], (DEPTH, A_GROUPS, CHUNK), f32),
        "q_norm": gain(ks[8], (DEPTH, HEAD_DIM)),
        "k_norm": gain(ks[9], (DEPTH, HEAD_DIM)),
        "w_br_a": nrm(ks[10], (DEPTH, A_WIDTH, D_MODEL), A_WIDTH),
        "w_br_b": nrm(ks[11], (DEPTH, B_WIDTH, D_MODEL), B_WIDTH),
        "w_out": nrm(ks[12], (DEPTH, D_MODEL, D_MODEL), D_MODEL),
        "norm_ffn": gain(ks[13], (DEPTH, D_MODEL)),
        "w_ffn_gate": nrm(ks[14], (DEPTH, D_MODEL, D_FF), D_MODEL),
        "w_ffn_up": nrm(ks[15], (DEPTH, D_MODEL, D_FF), D_MODEL),
        "w_ffn_down": nrm(ks[16], (DEPTH, D_FF, D_MODEL), D_FF),
        "norm_pl": gain(ks[17], (DEPTH, D_MODEL)),
        "w_pl_gate": nrm(ks[18], (DEPTH, D_MODEL, D_MODEL), D_MODEL),
        "w_pl_proj": nrm(ks[19], (DEPTH, PL_DIM, D_MODEL), PL_DIM),
    }


def reference(x, p, positions, norm_mix, w_in, v_ln_g, v_ln_b, w_spatial, b_spatial,
              q_norm, k_norm, w_br_a, w_br_b, w_out, norm_ffn, w_ffn_gate, w_ffn_up,
              w_ffn_down, norm_pl, w_pl_gate, w_pl_proj):
    B, S, _ = x.shape
    splits = [A_WIDTH, 2 * A_WIDTH, 2 * A_WIDTH + B_WIDTH, 2 * A_WIDTH + 2 * B_WIDTH,
              2 * A_WIDTH + 3 * B_WIDTH, 2 * A_WIDTH + 3 * B_WIDTH + D_MODEL]
    h = x
    for i in range(DEPTH):
        xn = rms_norm(h, norm_mix[i])
        z = xn @ w_in[i]
        u, va, q, k, v, g_a, g_b = jnp.split(z, splits, axis=-1)
        a_out = chunked_spatial_gating(u, va, v_ln_g[i], v_ln_b[i], w_spatial[i], b_spatial[i])
        q = q.reshape(B, S, B_HEADS, HEAD_DIM)
        k = k.reshape(B, S, B_HEADS, HEAD_DIM)
        v = v.reshape(B, S, B_HEADS, HEAD_DIM)
        q = partial_rotary(rms_norm(q, q_norm[i]), positions) * (HEAD_DIM ** -0.5)
        k = partial_rotary(rms_norm(k, k_norm[i]), positions)
        b_out = dilated_attention(q, k, v).reshape(B, S, B_WIDTH)
        merged = jax.nn.sigmoid(g_a) * (a_out @ w_br_a[i]) + jax.nn.sigmoid(g_b) * (b_out @ w_br_b[i])
        h = h + merged @ w_out[i]
        hn = rms_norm(h, norm_ffn[i])
        h = h + (jax.nn.silu(hn @ w_ffn_gate[i]) * (hn @ w_ffn_up[i])) @ w_ffn_down[i]
        hp = rms_norm(h, norm_pl[i])
        h = h + jax.nn.sigmoid(hp @ w_pl_gate[i]) * (p[i] @ w_pl_proj[i])
    return h
```

```python
import contextlib
import math
import numpy as np
import ml_dtypes
import concourse.bass as bass
import concourse.mybir as mybir
from concourse.bass_utils import run_bass_kernel_spmd

F32 = mybir.dt.float32
BF16 = mybir.dt.bfloat16
I32 = mybir.dt.int32
AF = mybir.ActivationFunctionType
ALU = mybir.AluOpType
AX = mybir.AxisListType

ENGS = ("pe", "act", "dve", "pool", "sp")
NDSEM = 24
NEG = -30000.0
EPS = 1e-6
T = 1024
NCORES = 8


class Region:
    __slots__ = ("name", "writer", "readers")

    def __init__(self, name=""):
        self.name = name
        self.writer = None
        self.readers = []


class Op:
    __slots__ = ("eng", "fn", "deps", "signal", "count", "is_dma", "dsem", "dcount")

    def __init__(self, eng, fn, is_dma):
        self.eng = eng
        self.fn = fn
        self.deps = []
        self.signal = False
        self.count = 0
        self.is_dma = is_dma
        self.dsem = None
        self.dcount = 0


class Prog:
    def __init__(self, nc, same_engine_sync=True):
        self.nc = nc
        self.ops = {e: [] for e in ENGS}
        self.same_engine_sync = same_engine_sync
        self.stack = contextlib.ExitStack()
        self.nops = 0

    def sbuf(self, name, shape, dt):
        return self.stack.enter_context(self.nc.sbuf_tensor(name, list(shape), dt))

    def psum(self, name, shape, dt=F32):
        return self.stack.enter_context(self.nc.psum_tensor(name, list(shape), dt))

    def op(self, eng, fn, reads=(), writes=(), dma=False):
        o = Op(eng, fn, dma)
        deps = []
        for r in reads:
            if r.writer is not None:
                deps.append(r.writer)
        for r in writes:
            if r.writer is not None:
                deps.append(r.writer)
            deps.extend(r.readers)
        seen = set()
        for d in deps:
            if id(d) in seen:
                continue
            seen.add(id(d))
            if not d.is_dma and not dma and d.eng == eng:
                if eng == "pe" or not self.same_engine_sync:
                    continue
            o.deps.append(d)
        for r in reads:
            if not dma:
                r.readers = [x for x in r.readers if x.is_dma or x.eng != eng]
            r.readers.append(o)
        for r in writes:
            r.writer = o
            r.readers = []
        self.ops[eng].append(o)
        self.nops += 1
        return o

    def emit(self):
        nc = self.nc
        st = self.stack
        engsem = {e: st.enter_context(nc.semaphore("s_" + e)) for e in ENGS}
        dsems = {
            e: [st.enter_context(nc.semaphore(f"d_{e}_{i}")) for i in range(NDSEM)]
            for e in ("sp", "pool")
        }
        for e in ENGS:
            for o in self.ops[e]:
                for d in o.deps:
                    if not d.is_dma:
                        d.signal = True
        for e in ENGS:
            c = 0
            k = 0
            dma_list = []
            for o in self.ops[e]:
                if o.is_dma:
                    o.dsem = dsems[e][k % NDSEM]
                    o.dcount = (k // NDSEM + 1) * 16
                    if k >= NDSEM:
                        o.deps.append(dma_list[k - NDSEM])
                    dma_list.append(o)
                    k += 1
                elif o.signal:
                    c += 1
                    o.count = c
        nwaits = {e: 0 for e in ENGS}

        def run(e, engobj):
            seen = {}
            for o in self.ops[e]:
                for d in o.deps:
                    if d.is_dma:
                        key, sem, val = id(d.dsem), d.dsem, d.dcount
                    else:
                        key, sem, val = d.eng, engsem[d.eng], d.count
                    if seen.get(key, 0) >= val:
                        continue
                    seen[key] = val
                    engobj.wait_ge(sem, val)
                    nwaits[e] += 1
                ins = o.fn(engobj)
                if o.is_dma:
                    ins.then_inc(o.dsem, 16)
                elif o.signal:
                    ins.then_inc(engsem[e], 1)

        with nc.Block() as block:

            @block.tensor
            def _(eng):
                run("pe", eng)

            @block.scalar
            def _(eng):
                run("act", eng)

            @block.vector
            def _(eng):
                run("dve", eng)

            @block.gpsimd
            def _(eng):
                run("pool", eng)

            @block.sync
            def _(eng):
                run("sp", eng)

        self.nwaits = nwaits
        st.close()


class Arena:
    def __init__(self, p, name, nblocks):
        self.t = p.sbuf(name, [128, nblocks * 1024], BF16)
        self.regs = [Region(f"{name}{i}") for i in range(nblocks)]
        self.n = nblocks * 1024

    def bf(self, off, n, parts=128):
        assert off + n <= self.n
        return self.t[0:parts, off:off + n]

    def f32(self, off, n, parts=128):
        assert off % 2 == 0 and off + 2 * n <= self.n
        return self.t[0:parts, off:off + 2 * n].bitcast(F32)

    def rg(self, off, nbf):
        return self.regs[off // 1024:(off + nbf - 1) // 1024 + 1]


class Ring:
    def __init__(self, items):
        self.items = items
        self.i = 0

    def next(self):
        x = self.items[self.i % len(self.items)]
        self.i += 1
        return x


class Cfg:
    def __init__(self, D=4096, AG=16, BH=16, FF=11008, PL=256, DEPTH=2, ffn_max=30):
        self.D, self.AG, self.BH, self.FF, self.PL, self.DEPTH = D, AG, BH, FF, PL, DEPTH
        self.KC = D // 128
        self.AW = AG * 128
        self.BW = BH * 128
        self.FC = FF // 128
        self.INW = 2 * self.AW + 3 * self.BW + 2 * D
        self.NP = 3 * self.KC + 2 * AG + 2
        nb = -(-self.FC // ffn_max)
        base = self.FC // nb
        rem = self.FC % nb
        self.ffn_blocks = []
        f = 0
        for i in range(nb):
            n = base + (1 if i < rem else 0)
            self.ffn_blocks.append((f, f + n))
            f += n


FULL = Cfg()

CB_ID, CB_ONES, CB_ROT, CB_MOMP, CB_MPH1, CB_MPH4, CB_MA16 = 0, 128, 256, 384, 640, 768, 896
NCB = 1024
CF_TRIL, CF_INVF = 0, 128
NCF = 129


import os as _os_mod


def _os_dbg(k):
    return bool(_os_mod.environ.get(k))


class Job:
    def __init__(self, kind, segs=None, epi=None, fn=None, pre=None, name=""):
        self.kind = kind
        self.segs = segs or []
        self.epi = epi
        self.fn = fn
        self.pre = pre
        self.name = name


class Seg:
    def __init__(self, W, nk, ncols, xin):
        self.W, self.nk, self.ncols, self.xin = W, nk, ncols, xin


def build(cfg, mode):
    D, KC, AG, BH, AW, BW, FC, PL = cfg.D, cfg.KC, cfg.AG, cfg.BH, cfg.AW, cfg.BW, cfg.FC, cfg.PL
    nc = bass.Bass("TRN2", target_bir_lowering=False)
    p = Prog(nc)
    layers = list(range(cfg.DEPTH)) if mode == "fused" else [0]
    NL = len(layers)

    def dram(name, shape, dt, kind):
        return nc.dram_tensor(name, list(shape), dt, kind=kind).ap()

    IN, OUT, INT = "ExternalInput", "ExternalOutput", "Internal"
    cst = dram("cst", [128, NCB + NCF], F32, IN)
    prm = dram("prm", [NL, 128, cfg.NP], F32, IN)
    W = {}
    if mode in ("fused", "A"):
        W["w_in"] = dram("w_in", [NL, D, cfg.INW], F32, IN)
        W["wsT"] = dram("wsT", [NL, AG, 128, 128], F32, IN)
        W["bsp"] = dram("bsp", [NL, 1, AG * 128], F32, IN)
        pos = dram("pos", [1, T], I32, IN)
    if mode in ("fused", "B"):
        for nm, shp in (("w_br_a", [AW, D]), ("w_br_b", [BW, D]), ("w_out", [D, D]),
                        ("w_ffn_gate", [D, cfg.FF]), ("w_ffn_up", [D, cfg.FF]), ("w_ffn_down", [cfg.FF, D]),
                        ("w_pl_gate", [D, D]), ("w_pl_proj", [PL, D])):
            W[nm] = dram(nm, [NL] + shp, F32, IN)
        pT = dram("pT", [NL, PL, T], F32, IN)
        nrow = dram("nrow", [NL, 2, 128], F32, IN)
    if mode == "fused":
        xT = dram("xT", [D, T], F32, IN)
        hT = dram("hT", [D, T], F32, OUT)
        qT = dram("qT", [BW, T], BF16, INT)
        gaT = dram("gaT", [D, T], BF16, INT)
        gbT = dram("gbT", [D, T], BF16, INT)
        raise NotImplementedError("fused exchange not wired yet")
    elif mode == "A":
        hin = dram("hin", [D, T], F32, IN)
        qT = dram("qT", [BW, T], BF16, OUT)
        kT = dram("kT", [BW, T], BF16, OUT)
        vO = dram("vO", [T, BW], BF16, OUT)
        gaT = dram("gaT", [D, T], BF16, OUT)
        gbT = dram("gbT", [D, T], BF16, OUT)
        aoT = dram("aoT", [AW, T], BF16, OUT)
    else:
        hin = dram("hin", [D, T], F32, IN)
        hT = dram("hT", [D, T], F32, OUT)
        qT = dram("qT", [BW, T], BF16, IN)
        khalo = dram("khalo", [3, BW, T], BF16, IN)
        vhalo = dram("vhalo", [3 * T, BW], BF16, IN)
        gaT = dram("gaT", [D, T], BF16, IN)
        gbT = dram("gbT", [D, T], BF16, IN)
        aoT = dram("aoT", [AW, T], BF16, IN)

    XA = Arena(p, "XA", 32)
    BC = Arena(p, "BC", 32)
    NS = 4
    wslots = [p.sbuf(f"ws{i}", [128, 2048], BF16) for i in range(NS)]
    wregs = [Region(f"ws{i}") for i in range(NS)]
    cbf = p.sbuf("cbf", [128, NCB], BF16)
    cbf_r = Region("cbf")
    cff = p.sbuf("cff", [128, NCF], F32)
    cff_r = Region("cff")
    prm_sb = p.sbuf("prm_sb", [128, NL * cfg.NP], F32)
    prm_r = Region("prm")
    rstd = p.sbuf("rstd", [128, T], F32)
    rstd_r = [Region("rstd0"), Region("rstd1")]
    smallf = p.sbuf("smallf", [128, 64], F32)
    small_r = Region("smallf")
    tmpf = Ring([(p.sbuf(f"tf{i}", [128, 512], F32), Region(f"tf{i}")) for i in range(6)])
    tmpb = Ring([(p.sbuf(f"tb{i}", [128, 512], BF16), Region(f"tb{i}")) for i in range(4)])
    MA = Arena(p, "MA", 16)
    banks = [p.psum(f"ps{i}", [128, 512], F32) for i in range(8)]
    bregs = [Region(f"ps{i}") for i in range(8)]

    ident = cbf[:, CB_ID:CB_ID + 128]
    ones_b = cbf[:, CB_ONES:CB_ONES + 128]
    rot_b = cbf[:, CB_ROT:CB_ROT + 128]

    def dma(q, out, in_, reads, writes, **kw):
        return p.op(q, lambda e: e.dma_start(out=out, in_=in_, **kw), reads=reads, writes=writes, dma=True)

    def actf(out, in_, func, reads, writes, bias=None, scale=None):
        kw = {}
        if bias is not None:
            kw["bias"] = bias
        if scale is not None:
            kw["scale"] = scale
        return p.op("act", lambda e: e.activation(out, in_, func, **kw), reads=reads, writes=writes)

    def tt(out, a, b, op, reads, writes, eng="dve"):
        return p.op(eng, lambda e: e.tensor_tensor(out, a, b, op), reads=reads, writes=writes)

    def ts(out, a, s1, s2, op0, op1, reads, writes):
        if s2 is None:
            return p.op("dve", lambda e: e.tensor_single_scalar(out, a, s1, op0), reads=reads, writes=writes)
        return p.op("dve", lambda e: e.tensor_scalar(out, a, s1, s2, op0, op1), reads=reads, writes=writes)

    def rsq(out, a, scale, reads, writes):
        p.op("act", lambda e: e.activation(out, a, AF.Sqrt, bias=EPS, scale=scale), reads=reads, writes=writes)
        p.op("dve", lambda e: e.reciprocal(out, out), reads=writes, writes=writes)

    def stt(out, a, s, b, op0, op1, reads, writes):
        return p.op("dve", lambda e: e.scalar_tensor_tensor(out, a, s, b, op0, op1), reads=reads, writes=writes)

    def mm(out, lhsT, rhs, start, stop, reads, writes, skip=False):
        if skip:
            return p.op("pe", lambda e: e.matmul(out, lhsT, rhs, start=start, stop=stop, skip_group_check=True),
                        reads=reads, writes=writes)
        return p.op("pe", lambda e: e.matmul(out, lhsT, rhs, start=start, stop=stop), reads=reads, writes=writes)

    def pcol(l, c0, n=1):
        b = l * cfg.NP + c0
        return prm_sb[:, b:b + n]

    P_NMIX, P_NFFN, P_NPL = 0, KC, 2 * KC
    P_LNG, P_LNB = 3 * KC, 3 * KC + AG
    P_GQ, P_GK = 3 * KC + 2 * AG, 3 * KC + 2 * AG + 1

    dma("pool", cbf[:], cst[:, 0:NCB], [], [cbf_r])
    dma("sp", cff[:], cst[:, NCB:NCB + NCF], [], [cff_r])
    dma("sp", prm_sb[:].rearrange("p (l n) -> p l n", l=NL), prm.rearrange("l p n -> p l n"), [], [prm_r])

    hT_r = [Region(f"hT{k}") for k in range(KC)]
    dr = {nm: Region(nm) for nm in ("qT", "kT", "vO", "gaT", "gbT", "aoT")}

    jobs = []

    def XN(k):
        return XA.bf(k * 1024, 1024), XA.rg(k * 1024, 1024)

    def norm_job(l, hsrc, hsrc_regs, gain0, copy_to=None):
        def fn():
            NSTG = 8
            stg = Ring([(BC.f32(i * 2048, 1024), BC.rg(i * 2048, 2048)) for i in range(NSTG)])
            ssb = [(banks[0], bregs[0]), (banks[1], bregs[1])]
            for k in range(KC):
                st, sr = stg.next()
                dma("sp", st, hsrc[k * 128:(k + 1) * 128, :], [hsrc_regs[k]], sr)
                if copy_to is not None:
                    dma("sp", copy_to[k * 128:(k + 1) * 128, :], st, sr, [hT_r[k]])
                for h in range(2):
                    sq, sqr = tmpb.next()
                    actf(sq[:], st[:, h * 512:(h + 1) * 512], AF.Square, sr, [sqr])
                    mm(ssb[h][0][:], ones_b, sq[:], k == 0, k == KC - 1, [sqr, cbf_r], [ssb[h][1]])
            for h in range(2):
                rs = rstd[:, h * 512:(h + 1) * 512]
                rsq(rs, ssb[h][0][:], 1.0 / D, [ssb[h][1]], [rstd_r[h]])
            src = copy_to if copy_to is not None else hsrc
            for k in range(KC):
                st, sr = stg.next()
                rr = [hT_r[k]] if copy_to is not None else [hsrc_regs[k]]
                dma("sp", st, src[k * 128:(k + 1) * 128, :], rr, sr)
                xa, xr = XN(k)
                stt(xa, st, pcol(l, gain0 + k), rstd[:], ALU.mult, ALU.mult, sr + rstd_r + [prm_r], xr)
        jobs.append(Job("special", fn=fn, name="norm"))

    if mode in ("fused", "A"):
        cosT = MA.f32(8192, T)
        sinT = MA.f32(10240, T)
        cs_rl = MA.rg(8192, 4096)

        def rope_tables():
            posi = MA.f32(0, T).bitcast(I32)
            ang = MA.f32(2048, T)
            y = MA.f32(4096, T)
            fr = MA.f32(6144, T)
            r_all = MA.rg(0, 8192)
            dma("sp", posi, pos.partition_broadcast(128).rearrange("p o t -> p (o t)"), [], r_all)
            p.op("dve", lambda e: e.tensor_copy(ang, posi), reads=r_all, writes=r_all)
            ts(ang, ang, cff[:, CF_INVF:CF_INVF + 1], None, ALU.mult, None, r_all + [cff_r], r_all)
            C1 = 6.28125
            C2 = 2.0 * math.pi - 6.28125
            for which, dst in ((0, sinT), (1, cosT)):
                if which == 1:
                    ts(ang, ang, math.pi / 2, None, ALU.add, None, r_all, r_all)
                ts(y, ang, 1.0 / (2 * math.pi), None, ALU.mult, None, r_all, r_all)
                p.op("dve", lambda e: e.tensor_copy(posi, y), reads=r_all, writes=r_all)
                p.op("dve", lambda e: e.tensor_copy(y, posi), reads=r_all, writes=r_all)
                stt(fr, y, -C1, ang, ALU.mult, ALU.add, r_all, r_all)
                stt(fr, y, -C2, fr, ALU.mult, ALU.add, r_all, r_all)
                ts(fr, fr, 3.1415925, -3.1415925, ALU.min, ALU.max, r_all, r_all)
                actf(dst, fr, AF.Sin, r_all, cs_rl)
        jobs.append(Job("special", fn=rope_tables, name="rope"))

    def p1(l, hsrc, hsrc_regs, copy_to):
        norm_job(l, hsrc, hsrc_regs, P_NMIX, copy_to)
        w_in = W["w_in"][l]
        c_u, c_va, c_q, c_k, c_v = 0, AW, 2 * AW, 2 * AW + BW, 2 * AW + 2 * BW
        c_ga, c_gb = 2 * AW + 3 * BW, 2 * AW + 3 * BW + D

        def prep():
            ts(smallf[:, 0:1], pcol(l, P_GQ), 128.0 ** -0.5, None, ALU.mult, None, [prm_r], [small_r])
        jobs.append(Job("special", fn=prep))

        qk_stage = Ring([(BC.bf((8 + i) * 1024, 1024), BC.rg((8 + i) * 1024, 1024)) for i in range(4)])

        def qk_jobs(col0, gain_ap, gain_regs, dst, dst_r):
            for hd in range(BH):
                def epi(job, bk, hd=hd):
                    stg, stg_r = qk_stage.next()
                    conts = []
                    for h in range(2):
                        pm, pmr = bk[h]
                        pa, par = bk[2 + h]
                        sq, sqr = tmpb.next()
                        actf(sq[:], pm[:], AF.Square, [pmr], [sqr])
                        mm(pa[:], ones_b, sq[:], True, True, [sqr, cbf_r], [par])
                        r1, r1r = tmpf.next()
                        rsq(r1[:], pa[:], 1.0 / 128, [par], [r1r])
                        qn, qnr = tmpb.next()
                        stt(qn[:], pm[:], gain_ap, r1[:], ALU.mult, ALU.mult, [pmr, r1r] + gain_regs, [qnr])
                        mm(pa[:], rot_b, qn[:], True, True, [qnr, cbf_r], [par])
                        t1, t1r = tmpf.next()
                        tt(t1[:], qn[:], cosT[:, h * 512:(h + 1) * 512], ALU.mult, [qnr] + cs_rl, [t1r])
                        t2, t2r = tmpf.next()
                        tt(t2[:], pa[:], sinT[:, h * 512:(h + 1) * 512], ALU.mult, [par] + cs_rl, [t2r])
                        tt(stg[:, h * 512:(h + 1) * 512], t1[:], t2[:], ALU.add, [t1r, t2r], stg_r)
                    dma("sp", dst[hd * 128:(hd + 1) * 128, :], stg, stg_r, [dst_r])
                seg = Seg(w_in[:, col0 + hd * 128:col0 + (hd + 1) * 128], KC, 128, XN)
                jobs.append(Job("fm", [seg], epi, name="qk"))

        qk_jobs(c_k, pcol(l, P_GK), [prm_r], kT, dr["kT"])
        qk_jobs(c_q, smallf[:, 0:1], [small_r], qT, dr["qT"])

        vstage = Ring([(BC.bf(i * 4096, 4096), BC.rg(i * 4096, 4096)) for i in range(2)])
        ncg_v = -(-BW // 512)
        for cg in range(ncg_v):
            ncols = min(512, BW - cg * 512)

            def epi_v(job, bk, cg=cg, ncols=ncols):
                stg, stg_r = vstage.next()
                sv = stg.rearrange("p (n e) -> p n e", e=512)
                for n in range(8):
                    if n % 2 == 0:
                        p.op("act", lambda e, n=n: e.copy(sv[:, n, 0:ncols], bk[n][0][:, 0:ncols]),
                             reads=[bk[n][1]], writes=stg_r)
                    else:
                        p.op("dve", lambda e, n=n: e.tensor_copy(sv[:, n, 0:ncols], bk[n][0][:, 0:ncols]),
                             reads=[bk[n][1]], writes=stg_r)
                dma("sp", vO.rearrange("(n p) e -> p n e", p=128)[:, :, cg * 512:cg * 512 + ncols],
                    sv[:, :, 0:ncols], stg_r, [dr["vO"]])
            seg = Seg(w_in[:, c_v + cg * 512:c_v + cg * 512 + ncols], KC, ncols, XN)
            jobs.append(Job("tm", [seg], epi_v, name="v"))

        ncg_a = -(-AW // 512)
        lnst = p.sbuf(f"lnst{l}", [128, 16 * ncg_a + 40], F32)
        lnst_r = Region("lnst")
        sums = lnst[:, 0:8 * ncg_a]
        sqs = lnst[:, 8 * ncg_a:16 * ncg_a]
        mean8 = lnst[:, 16 * ncg_a:16 * ncg_a + 8]
        rstd8 = lnst[:, 16 * ncg_a + 8:16 * ncg_a + 16]
        ex28 = lnst[:, 16 * ncg_a + 16:16 * ncg_a + 24]
        msq8 = lnst[:, 16 * ncg_a + 24:16 * ncg_a + 32]
        va = BC.bf(0, 8 * AW).rearrange("p (n e) -> p n e", e=AW)
        va_r = BC.rg(0, 8 * AW)
        for cg in range(ncg_a):
            ncols = min(512, AW - cg * 512)

            def epi_va(job, bk, cg=cg, ncols=ncols):
                for n in range(8):
                    ix = n * ncg_a + cg
                    p.op("act", lambda e, n=n, ix=ix: e.activation(va[:, n, cg * 512:cg * 512 + ncols], bk[n][0][:, 0:ncols],
                                                                   AF.Copy, accum_out=sums[:, ix:ix + 1]),
                         reads=[bk[n][1]], writes=va_r + [lnst_r])
                    jk, jkr = tmpb.next()
                    p.op("act", lambda e, n=n, ix=ix, jk=jk: e.activation(jk[:, 0:ncols], bk[n][0][:, 0:ncols],
                                                                          AF.Square, accum_out=sqs[:, ix:ix + 1]),
                         reads=[bk[n][1]], writes=[jkr, lnst_r])
            seg = Seg(w_in[:, c_va + cg * 512:c_va + cg * 512 + ncols], KC, ncols, XN)
            jobs.append(Job("tm", [seg], epi_va, name="va"))

        S_OFF = 16 * 1024

        def spatial():
            wsf = MA.f32(0, AG * 128).rearrange("p (g t) -> p g t", t=128)
            wsf_r = MA.rg(0, AG * 256)
            bsb = MA.f32(AG * 256, AG * 128).rearrange("p (g t) -> p g t", t=128)
            bsb_r = MA.rg(AG * 256, AG * 256)
            Bg = MA.f32(AG * 512, AG * 128).rearrange("p (g t) -> p g t", t=128)
            Bg_r = MA.rg(AG * 512, AG * 256)
            WTm = MA.bf(AG * 768, AG * 128).rearrange("p (g t) -> p g t", t=128)
            WTm_r = MA.rg(AG * 768, AG * 128)
            dma("sp", wsf, W["wsT"][l].rearrange("g s t -> s g t"), [], wsf_r)
            dma("sp", bsb.rearrange("p g t -> p (g t)"),
                W["bsp"][l].partition_broadcast(128).rearrange("p o n -> p (o n)"), [], bsb_r)
            for g in range(AG):
                tt(WTm[:, g, :], wsf[:, g, :], cff[:, CF_TRIL:CF_TRIL + 128], ALU.mult, wsf_r + [cff_r], WTm_r)
            for g in range(AG):
                b, br = banks[g // 4 % 2], bregs[g // 4 % 2]
                mm(b[:, (g % 4) * 128:(g % 4 + 1) * 128], ones_b, WTm[:, g, :], True, True, WTm_r + [cbf_r], [br])
                stt(Bg[:, g, :], b[:, (g % 4) * 128:(g % 4 + 1) * 128], pcol(l, P_LNB + g), bsb[:, g, :],
                    ALU.mult, ALU.add, [br, prm_r] + bsb_r, Bg_r)
            L = [lnst_r]
            p.op("dve", lambda e: e.tensor_reduce(mean8, sums.rearrange("p (n c) -> p n c", c=ncg_a), AX.X, ALU.add), reads=L, writes=L)
            p.op("dve", lambda e: e.tensor_reduce(ex28, sqs.rearrange("p (n c) -> p n c", c=ncg_a), AX.X, ALU.add), reads=L, writes=L)
            ts(mean8, mean8, 1.0 / AW, None, ALU.mult, None, L, L)
            ts(ex28, ex28, 1.0 / AW, None, ALU.mult, None, L, L)
            tt(msq8, mean8, mean8, ALU.mult, L, L)
            tt(ex28, ex28, msq8, ALU.subtract, L, L)
            rsq(rstd8, ex28, 1.0, L, L)
            for n in range(8):
                ts(va[:, n, :], va[:, n, :], mean8[:, n:n + 1], rstd8[:, n:n + 1],
                   ALU.subtract, ALU.mult, va_r + L, va_r)
            for g in range(AG):
                for half in range(2):
                    b, br = banks[2 + (2 * g + half) % 4], bregs[2 + (2 * g + half) % 4]
                    for nn in range(4):
                        n = half * 4 + nn
                        mm(b[:, nn * 128:(nn + 1) * 128], va[:, n, g * 128:(g + 1) * 128], WTm[:, g, :],
                           True, True, va_r + WTm_r, [br])
                    so = S_OFF + g * 1024 + half * 512
                    stt(BC.bf(so, 512).rearrange("p (n t) -> p n t", t=128),
                        b[:].rearrange("p (n t) -> p n t", t=128), pcol(l, P_LNG + g),
                        Bg[:, g, :].unsqueeze(1).to_broadcast([128, 4, 128]),
                        ALU.mult, ALU.add, [br, prm_r] + Bg_r, BC.rg(so, 512))
        jobs.append(Job("special", fn=spatial, name="spatial"))

        for m0 in range(0, AG, 2):
            nch = min(2, AG - m0)

            def epi_u(job, bk, m0=m0, nch=nch):
                for ci in range(nch):
                    for h in range(2):
                        so = S_OFF + (m0 + ci) * 1024 + h * 512
                        pm, pmr = bk[ci * 2 + h]
                        tt(BC.bf(so, 512), pm[:], BC.bf(so, 512), ALU.mult, [pmr] + BC.rg(so, 512), BC.rg(so, 512))
                if mode == "A":
                    dma("sp", aoT[m0 * 128:(m0 + nch) * 128, :].rearrange("(c p) t -> p c t", p=128),
                        BC.bf(S_OFF + m0 * 1024, nch * 1024).rearrange("p (c t) -> p c t", t=1024),
                        BC.rg(S_OFF + m0 * 1024, nch * 1024), [dr["aoT"]])
            seg = Seg(w_in[:, c_u + m0 * 128:c_u + (m0 + nch) * 128], KC, nch * 128, XN)
            jobs.append(Job("fm", [seg], epi_u, name="u"))

        gstage = Ring([(BC.bf(i * 2048, 2048), BC.rg(i * 2048, 2048)) for i in range(2)])
        for (c0, dst, dreg) in ((c_ga, gaT, dr["gaT"]), (c_gb, gbT, dr["gbT"])):
            for m0 in range(0, KC, 2):
                def epi_g(job, bk, m0=m0, dst=dst, dreg=dreg):
                    stg, stg_r = gstage.next()
                    for ci in range(2):
                        for h in range(2):
                            pm, pmr = bk[ci * 2 + h]
                            actf(stg[:, ci * 1024 + h * 512:ci * 1024 + (h + 1) * 512], pm[:], AF.Sigmoid, [pmr], stg_r)
                    dma("sp", dst[m0 * 128:(m0 + 2) * 128, :].rearrange("(c p) t -> p c t", p=128),
                        stg.rearrange("p (c t) -> p c t", t=1024), stg_r, [dreg])
                seg = Seg(w_in[:, c0 + m0 * 128:c0 + (m0 + 2) * 128], KC, 256, XN)
                jobs.append(Job("fm", [seg], epi_g, name="gate"))

    def p2(l, ksrc, ksrc_r, vsrc, vsrc_r):
        negc0 = smallf[:, 2:3]

        def shift():
            rows = MA.f32(0, 256, parts=1)
            rows_r = MA.rg(0, 512)
            dma("sp", rows.rearrange("p (a d) -> p a d", a=2), nrow[l:l + 1], [], rows_r)
            mx = MA.f32(512, 2, parts=1)
            p.op("dve", lambda e: e.tensor_reduce(mx, rows.rearrange("p (a d) -> p a d", a=2), AX.X, ALU.max,
                                                  apply_absolute_value=True), reads=rows_r, writes=rows_r)
            c0b = MA.bf(1024, 1, parts=1)
            c0r = MA.rg(1024, 1)
            stt(c0b, mx[:, 0:1], -math.sqrt(128.0), mx[:, 1:2], ALU.mult, ALU.mult, rows_r, c0r)
            mm(banks[7][:, 0:1], cbf[0:1, CB_ONES:CB_ONES + 128], c0b, True, True, c0r + [cbf_r], [bregs[7]])
            p.op("dve", lambda e: e.tensor_copy(negc0, banks[7][:, 0:1]), reads=[bregs[7]], writes=[small_r])
        jobs.append(Job("special", fn=shift, name="shift"))

        KB = [0, 6144]
        QB = [12288, 14336]
        V1, V4, V16A, V16B = 16384, 16384 + 2304, 16384 + 2304 + 3072, 16384 + 2304 + 3072 + 4096
        ET = Ring([(MA.bf(i * 256, 256), MA.rg(i * 256, 256)) for i in range(8)])
        Ob = [(banks[0], bregs[0]), (banks[1], bregs[1])]
        Db = [(banks[2], bregs[2]), (banks[3], bregs[3])]
        Sb = Ring([(banks[4], bregs[4]), (banks[5], bregs[5]), (banks[6], bregs[6])])
        NHP = BH // 2

        def load_kq(hp):
            kb = XA.bf(KB[hp % 2], 6144).rearrange("p (h j t) -> p h j t", h=2, j=3)
            kr = XA.rg(KB[hp % 2], 6144)
            for j in range(3):
                dma("sp", kb[:, :, j, :], ksrc(j)[hp * 256:(hp + 1) * 256, :].rearrange("(h d) t -> d h t", d=128),
                    ksrc_r, kr)
            qb = XA.bf(QB[hp % 2], 2048).rearrange("p (h t) -> p h t", h=2)
            dma("sp", qb, qT[hp * 256:(hp + 1) * 256, :].rearrange("(h d) t -> d h t", d=128),
                [dr["qT"]], XA.rg(QB[hp % 2], 2048))

        def load_v(hp, which):
            cs = slice(hp * 256, (hp + 1) * 256)
            if which == 1:
                dma("sp", XA.bf(V1, 2304).rearrange("p (j e) -> p j e", e=256),
                    vsrc[1920:1920 + 1152, cs].rearrange("(j i) e -> i j e", i=128), vsrc_r, XA.rg(V1, 2304))
            elif which == 4:
                v4 = XA.bf(V4, 3072).rearrange("p (j r e) -> p j r e", j=3, r=4)
                for j in range(3):
                    dma("sp", v4[:, j, :, :],
                        vsrc[1536 + 512 * j:1536 + 512 * (j + 1), cs].rearrange("(i r) e -> i r e", r=4),
                        vsrc_r, XA.rg(V4, 3072))
            else:
                dma("sp", XA.bf(V16A, 4096).rearrange("p (r e) -> p r e", e=256),
                    vsrc[0:2048, cs].rearrange("(i r) e -> i r e", r=16), vsrc_r, XA.rg(V16A, 4096))
                dma("sp", XA.bf(V16B, 4096, parts=64).rearrange("p (r e) -> p r e", e=256),
                    vsrc[2048:3072, cs].rearrange("(i r) e -> i r e", r=16), vsrc_r, XA.rg(V16B, 4096))

        def head(hp, hh):
            h = hp * 2 + hh
            kflat = XA.bf(KB[hp % 2] + hh * 3072, 3072)
            k_r = XA.rg(KB[hp % 2], 6144)
            qf = XA.bf(QB[hp % 2] + hh * 1024, 1024)
            q_r = XA.rg(QB[hp % 2], 2048)
            v1 = XA.bf(V1, 2304).rearrange("p (j e) -> p j e", e=256)
            v4 = XA.bf(V4, 3072).rearrange("p (j r e) -> p j r e", j=3, r=4)
            v16a = XA.bf(V16A, 4096).rearrange("p (r e) -> p r e", e=256)
            v16b = XA.bf(V16B, 4096, parts=64).rearrange("p (r e) -> p r e", e=256)
            es = slice(hh * 128, (hh + 1) * 128)
            for (b, br) in Ob + Db:
                p.op("dve", lambda e, b=b: e.memset(b[:], 0.0), writes=[br])
            tiles = []
            for j in range(9):
                kap = kflat[:, 1920 + 128 * j:1920 + 128 * (j + 1)]
                if j == 0:
                    qap, nq, mask = qf[:, 0:128], 128, cbf[:, CB_MPH1:CB_MPH1 + 128]
                    outs = [(0, slice(0, 128), slice(0, 128))]
                elif j == 8:
                    qap, nq, mask = qf[:, 896:1024], 128, cbf[:, CB_MOMP:CB_MOMP + 128]
                    outs = [(1, slice(384, 512), slice(0, 128))]
                else:
                    qap, nq, mask = qf[:, 128 * (j - 1):128 * (j + 1)], 256, cbf[:, CB_MOMP:CB_MOMP + 256]
                    outs = []
                    for t in range(2):
                        blk = j - 1 + t
                        outs.append((blk // 4, slice((blk % 4) * 128, (blk % 4 + 1) * 128), slice(t * 128, (t + 1) * 128)))
                tiles.append((128, kap, qap, nq, mask, v1[:, j, es], XA.rg(V1, 2304), outs))
            for r in range(4):
                for j in range(3):
                    kap = kflat[:, 1536 + 512 * j + r:1536 + 512 * (j + 1):4]
                    if j == 0:
                        qap, nq, mask = qf[:, r:512:4], 128, cbf[:, CB_MPH4:CB_MPH4 + 128]
                        outs = [(0, slice(r, 512, 4), slice(0, 128))]
                    elif j == 2:
                        qap, nq, mask = qf[:, 512 + r:1024:4], 128, cbf[:, CB_MOMP:CB_MOMP + 128]
                        outs = [(1, slice(r, 512, 4), slice(0, 128))]
                    else:
                        qap, nq, mask = qf[:, r:1024:4], 256, cbf[:, CB_MOMP:CB_MOMP + 256]
                        outs = [(0, slice(r, 512, 4), slice(0, 128)), (1, slice(r, 512, 4), slice(128, 256))]
                    tiles.append((128, kap, qap, nq, mask, v4[:, j, r, es], XA.rg(V4, 3072), outs))
            for r in range(16):
                qap = qf[:, r:1024:16]
                outs = [(0, slice(r, 512, 16), slice(0, 32)), (1, slice(r, 512, 16), slice(32, 64))]
                tiles.append((128, kflat[:, r:2048:16], qap, 64, cbf[:, CB_MA16:CB_MA16 + 64],
                              v16a[:, r, es], XA.rg(V16A, 4096), outs))
                tiles.append((64, kflat[:, 2048 + r:3072:16], qap, 64, cbf[0:64, CB_MOMP:CB_MOMP + 64],
                              v16b[:, r, es], XA.rg(V16B, 4096), outs))

            def s_stage(tl):
                nk, kap, qap, nq, mask, vap, vr, outs = tl
                sb, sbr = Sb.next()
                mm(sb[0:nk, 0:nq], kap, qap, True, False, k_r + q_r, [sbr])
                mm(sb[0:nk, 0:nq], cbf[0:nk, CB_ID:CB_ID + nk], mask, False, True, [cbf_r], [sbr])
                et, etr = ET.next()
                actf(et[0:nk, 0:nq], sb[0:nk, 0:nq], AF.Exp, [sbr, small_r], etr, bias=negc0[0:nk, :])
                return et, etr

            def pv_stage(tl, et, etr):
                nk, kap, qap, nq, mask, vap, vr, outs = tl
                for (bi, oc, ec) in outs:
                    mm(Ob[bi][0][:, oc], vap, et[0:nk, ec], False, False, vr + etr, [Ob[bi][1]], skip=True)
                    mm(Db[bi][0][:, oc], cbf[0:nk, CB_ONES:CB_ONES + 128], et[0:nk, ec], False, False,
                       etr + [cbf_r], [Db[bi][1]], skip=True)

            prev = None
            for tl in tiles:
                cur = (tl,) + s_stage(tl)
                if prev is not None:
                    pv_stage(*prev)
                prev = cur
            pv_stage(*prev)
            for half in range(2):
                rd, rdr = tmpf.next()
                p.op("dve", lambda e, rd=rd, half=half: e.reciprocal(rd[:], Db[half][0][:]), reads=[Db[half][1]], writes=[rdr])
                tt(BC.bf(h * 1024 + half * 512, 512), Ob[half][0][:], rd[:], ALU.mult, [Ob[half][1], rdr],
                   BC.rg(h * 1024 + half * 512, 512))

        def attn():
            load_kq(0)
            for w in (1, 4, 16):
                load_v(0, w)
            for hp in range(NHP):
                if hp + 1 < NHP:
                    load_kq(hp + 1)
                head(hp, 0)
                head(hp, 1)
                if hp + 1 < NHP:
                    for w in (1, 4, 16):
                        load_v(hp + 1, w)
        jobs.append(Job("special", fn=attn, name="attn"))

    ostage = Ring([(MA.f32(8192 + i * 4096, 2048), MA.rg(8192 + i * 4096, 4096)) for i in range(2)])

    def acc_epi(m0, nch):
        def epi(job, bk):
            stg, stg_r = ostage.next()
            for ci in range(nch):
                for h in range(2):
                    pm, pmr = bk[ci * 2 + h]
                    dst = stg[:, ci * 1024 + h * 512:ci * 1024 + (h + 1) * 512]
                    if h == 0:
                        p.op("act", lambda e, dst=dst, pm=pm: e.copy(dst, pm[:]), reads=[pmr], writes=stg_r)
                    else:
                        p.op("dve", lambda e, dst=dst, pm=pm: e.tensor_copy(dst, pm[:]), reads=[pmr], writes=stg_r)
            dma("pool", hT[m0 * 128:(m0 + nch) * 128, :].rearrange("(c p) t -> p c t", p=128),
                stg[:, 0:nch * 1024].rearrange("p (c t) -> p c t", t=1024), stg_r,
                [hT_r[m0 + c] for c in range(nch)], accum_op=ALU.add)
        return epi

    def p3(l):
        def AO(k):
            return BC.bf(16 * 1024 + k * 1024, 1024), BC.rg(16 * 1024 + k * 1024, 1024)

        def BO(k):
            return BC.bf(k * 1024, 1024), BC.rg(k * 1024, 1024)

        gt = Ring([(MA.bf(i * 2048, 2048), MA.rg(i * 2048, 2048)) for i in range(2)])
        for m in range(KC):
            st8 = {}

            def pre(job, m=m, st8=st8):
                g, gr = gt.next()
                dma("sp", g[:, 0:1024], gaT[m * 128:(m + 1) * 128, :], [dr["gaT"]], gr)
                dma("sp", g[:, 1024:2048], gbT[m * 128:(m + 1) * 128, :], [dr["gbT"]], gr)
                st8["g"] = (g, gr)

            def epi(job, bk, m=m, st8=st8):
                g, gr = st8["g"]
                xa, xr = XN(m)
                for h in range(2):
                    t1, t1r = tmpf.next()
                    tt(t1[:], bk[h][0][:], g[:, h * 512:(h + 1) * 512], ALU.mult, [bk[h][1]] + gr, [t1r])
                    t2, t2r = tmpf.next()
                    tt(t2[:], bk[2 + h][0][:], g[:, 1024 + h * 512:1024 + (h + 1) * 512], ALU.mult, [bk[2 + h][1]] + gr, [t2r])
                    tt(xa[:, h * 512:(h + 1) * 512], t1[:], t2[:], ALU.add, [t1r, t2r], xr)
            segs = [Seg(W["w_br_a"][l][:, m * 128:(m + 1) * 128], AG, 128, AO),
                    Seg(W["w_br_b"][l][:, m * 128:(m + 1) * 128], BH, 128, BO)]
            jobs.append(Job("fm", segs, epi, pre=pre, name="merge"))
        for m0 in range(0, KC, 2):
            jobs.append(Job("fm", [Seg(W["w_out"][l][:, m0 * 128:(m0 + 2) * 128], KC, 256, XN)], acc_epi(m0, 2), name="wout"))
        norm_job(l, hT, hT_r, P_NFFN)
        for (f0, f1) in cfg.ffn_blocks:
            def ACT_(k):
                return BC.bf(k * 1024, 1024), BC.rg(k * 1024, 1024)
            for f in range(f0, f1):
                def epi(job, bk, f=f, f0=f0):
                    for h in range(2):
                        sg, sgr = tmpf.next()
                        actf(sg[:], bk[h][0][:], AF.Silu, [bk[h][1]], [sgr])
                        so = (f - f0) * 1024 + h * 512
                        tt(BC.bf(so, 512), bk[2 + h][0][:], sg[:], ALU.mult, [bk[2 + h][1], sgr], BC.rg(so, 512))
                segs = [Seg(W["w_ffn_gate"][l][:, f * 128:(f + 1) * 128], KC, 128, XN),
                        Seg(W["w_ffn_up"][l][:, f * 128:(f + 1) * 128], KC, 128, XN)]
                jobs.append(Job("fm", segs, epi, name="ffn_gu"))
            for m0 in range(0, KC, 2):
                jobs.append(Job("fm", [Seg(W["w_ffn_down"][l][f0 * 128:f1 * 128, m0 * 128:(m0 + 2) * 128], f1 - f0, 256, ACT_)],
                                acc_epi(m0, 2), name="ffn_down"))
        norm_job(l, hT, hT_r, P_NPL)
        pTb = MA.bf(4096, (PL // 128) * 1024)
        pTb_r = MA.rg(4096, (PL // 128) * 1024)

        def loadp():
            dma("pool", pTb.rearrange("p (c t) -> p c t", t=1024), pT[l].rearrange("(c p) t -> p c t", p=128), [], pTb_r)
        jobs.append(Job("special", fn=loadp))

        def PT(k):
            return pTb[:, k * 1024:(k + 1) * 1024], pTb_r
        for m in range(KC):
            def epi(job, bk, m=m):
                stg, stg_r = ostage.next()
                for h in range(2):
                    sg, sgr = tmpf.next()
                    actf(sg[:], bk[h][0][:], AF.Sigmoid, [bk[h][1]], [sgr])
                    tt(stg[:, h * 512:(h + 1) * 512], bk[2 + h][0][:], sg[:], ALU.mult, [bk[2 + h][1], sgr], stg_r)
                dma("pool", hT[m * 128:(m + 1) * 128, :], stg[:, 0:1024], stg_r, [hT_r[m]], accum_op=ALU.add)
            segs = [Seg(W["w_pl_gate"][l][:, m * 128:(m + 1) * 128], KC, 128, XN),
                    Seg(W["w_pl_proj"][l][:, m * 128:(m + 1) * 128], PL // 128, 128, PT)]
            jobs.append(Job("fm", segs, epi, name="pl"))

    hin_r = [Region(f"hin{k}") for k in range(KC)]
    if mode == "A":
        p1(0, hin, hin_r, None)
    elif mode == "B":
        def init():
            stg = Ring([(BC.f32(i * 2048, 1024), BC.rg(i * 2048, 2048)) for i in range(8)])
            for k in range(KC):
                st, sr = stg.next()
                dma("sp", st, hin[k * 128:(k + 1) * 128, :], [hin_r[k]], sr)
                dma("sp", hT[k * 128:(k + 1) * 128, :], st, sr, [hT_r[k]])
        jobs.append(Job("special", fn=init))
        kh_r = [Region("khalo")]
        vh_r = [Region("vhalo")]
        p2(0, lambda j: khalo[j], kh_r, vhalo, vh_r)

        def load_ao():
            dma("sp", BC.bf(16 * 1024, AG * 1024).rearrange("p (c t) -> p c t", t=1024),
                aoT.rearrange("(c p) t -> p c t", p=128), [dr["aoT"]], BC.rg(16 * 1024, AG * 1024))
        jobs.append(Job("special", fn=load_ao))
        p3(0)

    import os as _os
    _ks = int(_os.environ.get("KSTOP", "0"))
    if _ks:
        jobs = jobs[:_ks]
    entries = []
    njob = 0
    for jb in jobs:
        if jb.kind == "special":
            entries.append(("special", jb))
            continue
        if jb.kind == "fm":
            jb.set = njob % 2
            njob += 1
        first = True
        for si, sg in enumerate(jb.segs):
            ks = 2048 // (sg.ncols if jb.kind == "fm" else 512)
            k0 = 0
            while k0 < sg.nk:
                nk = min(ks, sg.nk - k0)
                entries.append(("slab", jb, si, k0, nk, first))
                first = False
                k0 += nk
        entries.append(("epi", jb))

    slab_ids = [i for i, e in enumerate(entries) if e[0] == "slab"]
    slab_slot = {}
    for n, i in enumerate(slab_ids):
        slab_slot[i] = n % NS
    LA = NS - 1
    state = {"next": 0}

    def slot_view(i):
        _, jb, si, k0, nk, _ = entries[i]
        sg = jb.segs[si]
        return wslots[slab_slot[i]][:, 0:nk * sg.ncols].rearrange("p (k m) -> p k m", m=sg.ncols)

    def issue_loads(upto):
        while state["next"] < min(upto, len(slab_ids)):
            i = slab_ids[state["next"]]
            _, jb, si, k0, nk, _ = entries[i]
            sg = jb.segs[si]
            dma("pool", slot_view(i), sg.W[k0 * 128:(k0 + nk) * 128, :].rearrange("(k p) m -> p k m", p=128),
                [], [wregs[slab_slot[i]]])
            state["next"] += 1

    ordinal = {i: n for n, i in enumerate(slab_ids)}
    for i, ent in enumerate(entries):
        if ent[0] == "special":
            nxt = [n for n, j in enumerate(slab_ids) if j > i]
            if nxt:
                issue_loads(nxt[0] + LA)
            ent[1].fn()
        elif ent[0] == "slab":
            _, jb, si, k0, nk, first = ent
            issue_loads(ordinal[i] + LA + 1)
            if first and jb.pre is not None:
                jb.pre(jb)
            sg = jb.segs[si]
            sv = slot_view(i)
            wr = wregs[slab_slot[i]]
            if jb.kind == "fm":
                chunk0 = sum(s.ncols // 128 for s in jb.segs[:si])
                for kk in range(nk):
                    k = k0 + kk
                    xap, xr = sg.xin(k)
                    for ci in range(sg.ncols // 128):
                        for h in range(2):
                            b = jb.set * 4 + (chunk0 + ci) * 2 + h
                            mm(banks[b][:], sv[:, kk, ci * 128:(ci + 1) * 128], xap[:, h * 512:(h + 1) * 512],
                               k == 0, k == sg.nk - 1, [wr] + xr, [bregs[b]])
            else:
                for kk in range(nk):
                    k = k0 + kk
                    xap, xr = sg.xin(k)
                    for n in range(8):
                        mm(banks[n][:, 0:sg.ncols], xap[:, n * 128:(n + 1) * 128], sv[:, kk, :],
                           k == 0, k == sg.nk - 1, [wr] + xr, [bregs[n]])
        else:
            jb = ent[1]
            if jb.kind == "fm":
                bk = [(banks[jb.set * 4 + j], bregs[jb.set * 4 + j]) for j in range(4)]
            else:
                bk = [(banks[j], bregs[j]) for j in range(8)]
            jb.epi(jb, bk)

    if mode == "A":
        outs = list(dr.values())
    else:
        outs = hT_r
    p.op("sp", lambda e: e.nop(), reads=outs)
    p.emit()
    return nc, p


def make_consts(core):
    cb = np.zeros((128, NCB), np.float32)
    i = np.arange(128)[:, None]
    j = np.arange(128)[None, :]
    cb[:, CB_ID:CB_ID + 128] = np.eye(128)
    cb[:, CB_ONES:CB_ONES + 128] = 1.0
    rot = np.zeros((128, 128), np.float32)
    for d in range(16):
        rot[d + 16, d] = -1.0
        rot[d, d + 16] = 1.0
    cb[:, CB_ROT:CB_ROT + 128] = rot
    MO = np.where(i <= j, 0.0, NEG)
    MP = np.where(i >= j, 0.0, NEG)
    cb[:, CB_MOMP:CB_MOMP + 128] = MO
    cb[:, CB_MOMP + 128:CB_MOMP + 256] = MP
    allneg = np.full((128, 128), NEG)
    cb[:, CB_MPH1:CB_MPH1 + 128] = allneg if core == 0 else MP
    cb[:, CB_MPH4:CB_MPH4 + 128] = allneg if core == 0 else MP
    ma = MP[:, :64].copy()
    if core == 0:
        ma[:] = NEG
    elif core == 1:
        ma[:64, :] = NEG
    cb[:, CB_MA16:CB_MA16 + 64] = ma
    cf = np.zeros((128, NCF), np.float32)
    cf[:, CF_TRIL:CF_TRIL + 128] = (i <= j)
    half = 16
    invf = np.power(np.float32(500000.0), -np.arange(half, dtype=np.float32) / np.float32(half)).astype(np.float32)
    col = np.zeros(128, np.float32)
    col[:16] = invf
    col[16:32] = invf
    cf[:, CF_INVF] = col
    return np.concatenate([cb, cf], axis=1)


def col_layout(v):
    return np.ascontiguousarray(v.reshape(-1, 128).T)


def make_prm(cfg, inp, l):
    parts = [col_layout(inp["norm_mix"][l]), col_layout(inp["norm_ffn"][l]), col_layout(inp["norm_pl"][l]),
             col_layout(inp["v_ln_g"][l]), col_layout(inp["v_ln_b"][l]),
             inp["q_norm"][l].reshape(128, 1), inp["k_norm"][l].reshape(128, 1)]
    return np.ascontiguousarray(np.concatenate(parts, axis=1).astype(np.float32))


_CACHE = {}


def get_prog(cfg, mode):
    key = (cfg.D, cfg.AG, cfg.BH, cfg.FF, mode)
    if key not in _CACHE:
        _CACHE[key] = build(cfg, mode)[0]
    return _CACHE[key]


def run_unfused(cfg, inp):
    inp = {k: np.asarray(v) for k, v in inp.items()}
    S = NCORES * T
    x = inp["x"].reshape(S, cfg.D)
    hT = [np.ascontiguousarray(x[c * T:(c + 1) * T].T) for c in range(NCORES)]
    consts = [make_consts(c) for c in range(NCORES)]
    pos = inp["positions"].reshape(S).astype(np.int32)
    ncA = get_prog(cfg, "A")
    ncB = get_prog(cfg, "B")
    bf = ml_dtypes.bfloat16
    for l in range(cfg.DEPTH):
        prm = make_prm(cfg, inp, l)[None]
        wsT = np.ascontiguousarray(inp["w_spatial"][l].transpose(0, 2, 1))[None]
        common_a = {
            "prm": prm, "w_in": inp["w_in"][l][None], "wsT": wsT,
            "bsp": inp["b_spatial"][l].reshape(1, 1, -1),
        }
        maps = [dict(common_a, cst=consts[c], hin=hT[c], pos=pos[c * T:(c + 1) * T].reshape(1, T)) for c in range(NCORES)]
        ra = run_bass_kernel_spmd(ncA, maps, core_ids=list(range(NCORES))).results
        zk = np.zeros((cfg.BW, T), bf)
        zv = np.zeros((T, cfg.BW), bf)
        common_b = {"prm": prm, "nrow": np.stack([inp["q_norm"][l], inp["k_norm"][l]])[None]}
        for nm in ("w_br_a", "w_br_b", "w_out", "w_ffn_gate", "w_ffn_up", "w_ffn_down", "w_pl_gate", "w_pl_proj"):
            common_b[nm] = inp[nm][l][None]
        maps = []
        for c in range(NCORES):
            kh = np.stack([ra[c - 2 + j]["kT"] if c - 2 + j >= 0 else zk for j in range(3)])
            vh = np.concatenate([ra[c - 2 + j]["vO"] if c - 2 + j >= 0 else zv for j in range(3)], axis=0)
            pTl = np.ascontiguousarray(inp["p"][l].reshape(S, cfg.PL)[c * T:(c + 1) * T].T)[None]
            maps.append(dict(common_b, cst=consts[c], hin=hT[c], qT=ra[c]["qT"], khalo=kh, vhalo=vh,
                             gaT=ra[c]["gaT"], gbT=ra[c]["gbT"], aoT=ra[c]["aoT"], pT=pTl))
        rb = run_bass_kernel_spmd(ncB, maps, core_ids=list(range(NCORES))).results
        hT = [rb[c]["hT"] for c in range(NCORES)]
    out = np.concatenate([h.T for h in hT], axis=0).reshape(1, S, cfg.D)
    return np.ascontiguousarray(out.astype(np.float32))


def kernel(**inputs):
    return run_unfused(FULL, inputs)
```

```python
import contextlib
import math
import numpy as np
import ml_dtypes
import concourse.bass as bass
import concourse.mybir as mybir
from concourse.bass_utils import run_bass_kernel_spmd

F32 = mybir.dt.float32
BF16 = mybir.dt.bfloat16
I32 = mybir.dt.int32
AF = mybir.ActivationFunctionType
ALU = mybir.AluOpType
AX = mybir.AxisListType

ENGS = ("pe", "act", "dve", "pool", "sp")
NDSEM = 24
NEG = -30000.0
EPS = 1e-6
T = 1024
NCORES = 8


class Region:
    __slots__ = ("name", "writer", "readers")

    def __init__(self, name=""):
        self.name = name
        self.writer = None
        self.readers = []


class Op:
    __slots__ = ("eng", "fn", "deps", "signal", "count", "is_dma", "dsem", "dcount", "cc")

    def __init__(self, eng, fn, is_dma):
        self.cc = False
        self.eng = eng
        self.fn = fn
        self.deps = []
        self.signal = False
        self.count = 0
        self.is_dma = is_dma
        self.dsem = None
        self.dcount = 0


class Prog:
    def __init__(self, nc, same_engine_sync=True):
        self.nc = nc
        self.ops = {e: [] for e in ENGS}
        self.same_engine_sync = same_engine_sync
        self.stack = contextlib.ExitStack()
        self.nops = 0

    def sbuf(self, name, shape, dt):
        return self.stack.enter_context(self.nc.sbuf_tensor(name, list(shape), dt))

    def psum(self, name, shape, dt=F32):
        return self.stack.enter_context(self.nc.psum_tensor(name, list(shape), dt))

    def op(self, eng, fn, reads=(), writes=(), dma=False, cc=False):
        o = Op(eng, fn, dma)
        o.cc = cc
        deps = []
        for r in reads:
            if r.writer is not None:
                deps.append(r.writer)
        for r in writes:
            if r.writer is not None:
                deps.append(r.writer)
            deps.extend(r.readers)
        seen = set()
        for d in deps:
            if id(d) in seen:
                continue
            seen.add(id(d))
            if not d.is_dma and not dma and d.eng == eng:
                if eng == "pe" or not self.same_engine_sync:
                    continue
            o.deps.append(d)
        for r in reads:
            if not dma:
                r.readers = [x for x in r.readers if x.is_dma or x.eng != eng]
            r.readers.append(o)
        for r in writes:
            r.writer = o
            r.readers = []
        self.ops[eng].append(o)
        self.nops += 1
        return o

    def emit(self):
        nc = self.nc
        st = self.stack
        engsem = {e: st.enter_context(nc.semaphore("s_" + e)) for e in ENGS}
        dsems = {
            e: [st.enter_context(nc.semaphore(f"d_{e}_{i}")) for i in range(NDSEM)]
            for e in ("sp", "pool")
        }
        for e in ENGS:
            for o in self.ops[e]:
                for d in o.deps:
                    if not d.is_dma:
                        d.signal = True
        ccsems = [st.enter_context(nc.semaphore(f"cc_{i}")) for i in range(4)]
        for e in ENGS:
            c = 0
            k = 0
            kc = 0
            dma_list = []
            for o in self.ops[e]:
                if o.cc:
                    o.dsem = ccsems[kc % 4]
                    o.dcount = kc // 4 + 1
                    kc += 1
                elif o.is_dma:
                    o.dsem = dsems[e][k % NDSEM]
                    o.dcount = (k // NDSEM + 1) * 16
                    if k >= NDSEM:
                        o.deps.append(dma_list[k - NDSEM])
                    dma_list.append(o)
                    k += 1
                elif o.signal:
                    c += 1
                    o.count = c
        nwaits = {e: 0 for e in ENGS}

        def run(e, engobj):
            seen = {}
            for o in self.ops[e]:
                for d in o.deps:
                    if d.is_dma:
                        key, sem, val = id(d.dsem), d.dsem, d.dcount
                    else:
                        key, sem, val = d.eng, engsem[d.eng], d.count
                    if seen.get(key, 0) >= val:
                        continue
                    seen[key] = val
                    engobj.wait_ge(sem, val)
                    nwaits[e] += 1
                ins = o.fn(engobj)
                if o.cc:
                    ins.then_inc(o.dsem, 1)
                elif o.is_dma:
                    ins.then_inc(o.dsem, 16)
                elif o.signal:
                    ins.then_inc(engsem[e], 1)

        with nc.Block() as block:

            @block.tensor
            def _(eng):
                run("pe", eng)

            @block.scalar
            def _(eng):
                run("act", eng)

            @block.vector
            def _(eng):
                run("dve", eng)

            @block.gpsimd
            def _(eng):
                run("pool", eng)

            @block.sync
            def _(eng):
                run("sp", eng)

        self.nwaits = nwaits
        st.close()


class Arena:
    def __init__(self, p, name, nblocks):
        self.t = p.sbuf(name, [128, nblocks * 1024], BF16)
        self.regs = [Region(f"{name}{i}") for i in range(nblocks)]
        self.n = nblocks * 1024

    def bf(self, off, n, parts=128):
        assert off + n <= self.n
        return self.t[0:parts, off:off + n]

    def f32(self, off, n, parts=128):
        assert off % 2 == 0 and off + 2 * n <= self.n
        return self.t[0:parts, off:off + 2 * n].bitcast(F32)

    def rg(self, off, nbf):
        return self.regs[off // 1024:(off + nbf - 1) // 1024 + 1]


class Ring:
    def __init__(self, items):
        self.items = items
        self.i = 0

    def next(self):
        x = self.items[self.i % len(self.items)]
        self.i += 1
        return x


class Cfg:
    def __init__(self, D=4096, AG=16, BH=16, FF=11008, PL=256, DEPTH=2, ffn_max=30):
        self.D, self.AG, self.BH, self.FF, self.PL, self.DEPTH = D, AG, BH, FF, PL, DEPTH
        self.KC = D // 128
        self.AW = AG * 128
        self.BW = BH * 128
        self.FC = FF // 128
        self.INW = 2 * self.AW + 3 * self.BW + 2 * D
        self.NP = 3 * self.KC + 2 * AG + 2
        nb = -(-self.FC // ffn_max)
        base = self.FC // nb
        rem = self.FC % nb
        self.ffn_blocks = []
        f = 0
        for i in range(nb):
            n = base + (1 if i < rem else 0)
            self.ffn_blocks.append((f, f + n))
            f += n


FULL = Cfg()

CB_ID, CB_ONES, CB_ROT, CB_MOMP, CB_MPH1, CB_MPH4, CB_MA16 = 0, 128, 256, 384, 640, 768, 896
NCB = 1024
CF_TRIL, CF_INVF = 0, 128
NCF = 129


import os as _os_mod


def _os_dbg(k):
    return bool(_os_mod.environ.get(k))


class Job:
    def __init__(self, kind, segs=None, epi=None, fn=None, pre=None, name=""):
        self.kind = kind
        self.segs = segs or []
        self.epi = epi
        self.fn = fn
        self.pre = pre
        self.name = name


class Seg:
    def __init__(self, W, nk, ncols, xin):
        self.W, self.nk, self.ncols, self.xin = W, nk, ncols, xin


def build(cfg, mode):
    D, KC, AG, BH, AW, BW, FC, PL = cfg.D, cfg.KC, cfg.AG, cfg.BH, cfg.AW, cfg.BW, cfg.FC, cfg.PL
    nc = bass.Bass("TRN2", target_bir_lowering=False)
    p = Prog(nc)
    layers = list(range(cfg.DEPTH)) if mode == "fused" else [0]
    NL = len(layers)

    def dram(name, shape, dt, kind):
        return nc.dram_tensor(name, list(shape), dt, kind=kind).ap()

    IN, OUT, INT = "ExternalInput", "ExternalOutput", "Internal"
    cst = dram("cst", [128, NCB + NCF], F32, IN)
    prm = dram("prm", [NL, 128, cfg.NP], F32, IN)
    W = {}
    if mode in ("fused", "A"):
        W["w_in"] = dram("w_in", [NL, D, cfg.INW], F32, IN)
        W["wsT"] = dram("wsT", [NL, AG, 128, 128], F32, IN)
        W["bsp"] = dram("bsp", [NL, 1, AG * 128], F32, IN)
        pos = dram("pos", [1, T], I32, IN)
    if mode in ("fused", "B"):
        for nm, shp in (("w_br_a", [AW, D]), ("w_br_b", [BW, D]), ("w_out", [D, D]),
                        ("w_ffn_gate", [D, cfg.FF]), ("w_ffn_up", [D, cfg.FF]), ("w_ffn_down", [cfg.FF, D]),
                        ("w_pl_gate", [D, D]), ("w_pl_proj", [PL, D])):
            W[nm] = dram(nm, [NL] + shp, F32, IN)
        pT = dram("pT", [NL, PL, T], F32, IN)
        nrow = dram("nrow", [NL, 2, 128], F32, IN)
    if mode == "fused":
        xT = dram("xT", [D, T], F32, IN)
        hT = dram("hT", [D, T], F32, OUT)
        qT = dram("qT", [BW, T], BF16, INT)
        gaT = dram("gaT", [D, T], BF16, INT)
        gbT = dram("gbT", [D, T], BF16, INT)
        aoT = None
        xidx = dram("xidx", [128, 48], I32, IN)
        kown = [dram(f"kown{l}", [BW, T], BF16, INT) for l in layers]
        vown = [dram(f"vown{l}", [T, BW], BF16, INT) for l in layers]
        kall = [dram(f"kall{l}", [NCORES * BW, T], BF16, INT) for l in layers]
        vall = [dram(f"vall{l}", [NCORES * T, BW], BF16, INT) for l in layers]
        khalo = dram("khalo", [3, BW, T], BF16, INT)
        vhalo = dram("vhalo", [3 * T, BW], BF16, INT)
    elif mode == "A":
        hin = dram("hin", [D, T], F32, IN)
        qT = dram("qT", [BW, T], BF16, OUT)
        kT = dram("kT", [BW, T], BF16, OUT)
        vO = dram("vO", [T, BW], BF16, OUT)
        gaT = dram("gaT", [D, T], BF16, OUT)
        gbT = dram("gbT", [D, T], BF16, OUT)
        aoT = dram("aoT", [AW, T], BF16, OUT)
    else:
        hin = dram("hin", [D, T], F32, IN)
        hT = dram("hT", [D, T], F32, OUT)
        qT = dram("qT", [BW, T], BF16, IN)
        khalo = dram("khalo", [3, BW, T], BF16, IN)
        vhalo = dram("vhalo", [3 * T, BW], BF16, IN)
        gaT = dram("gaT", [D, T], BF16, IN)
        gbT = dram("gbT", [D, T], BF16, IN)
        aoT = dram("aoT", [AW, T], BF16, IN)

    XA = Arena(p, "XA", 32)
    BC = Arena(p, "BC", 32)
    NS = 4
    wslots = [p.sbuf(f"ws{i}", [128, 2048], BF16) for i in range(NS)]
    wregs = [Region(f"ws{i}") for i in range(NS)]
    cbf = p.sbuf("cbf", [128, NCB], BF16)
    cbf_r = Region("cbf")
    cff = p.sbuf("cff", [128, NCF], F32)
    cff_r = Region("cff")
    prm_sb = p.sbuf("prm_sb", [128, NL * cfg.NP], F32)
    prm_r = Region("prm")
    rstd = p.sbuf("rstd", [128, T], F32)
    rstd_r = [Region("rstd0"), Region("rstd1")]
    smallf = p.sbuf("smallf", [128, 64], F32)
    small_r = Region("smallf")
    tmpf = Ring([(p.sbuf(f"tf{i}", [128, 512], F32), Region(f"tf{i}")) for i in range(6)])
    tmpb = Ring([(p.sbuf(f"tb{i}", [128, 512], BF16), Region(f"tb{i}")) for i in range(4)])
    MA = Arena(p, "MA", 16)
    if mode in ("fused", "B"):
        et_tiles = [p.sbuf(f"et{i}", [128, 512], BF16) for i in range(4)]
    banks = [p.psum(f"ps{i}", [128, 512], F32) for i in range(8)]
    bregs = [Region(f"ps{i}") for i in range(8)]

    ident = cbf[:, CB_ID:CB_ID + 128]
    ones_b = cbf[:, CB_ONES:CB_ONES + 128]
    rot_b = cbf[:, CB_ROT:CB_ROT + 128]

    def dma(q, out, in_, reads, writes, **kw):
        return p.op(q, lambda e: e.dma_start(out=out, in_=in_, **kw), reads=reads, writes=writes, dma=True)

    def actf(out, in_, func, reads, writes, bias=None, scale=None):
        kw = {}
        if bias is not None:
            kw["bias"] = bias
        if scale is not None:
            kw["scale"] = scale
        return p.op("act", lambda e: e.activation(out, in_, func, **kw), reads=reads, writes=writes)

    def tt(out, a, b, op, reads, writes, eng="dve"):
        return p.op(eng, lambda e: e.tensor_tensor(out, a, b, op), reads=reads, writes=writes)

    def ts(out, a, s1, s2, op0, op1, reads, writes):
        if s2 is None:
            return p.op("dve", lambda e: e.tensor_single_scalar(out, a, s1, op0), reads=reads, writes=writes)
        return p.op("dve", lambda e: e.tensor_scalar(out, a, s1, s2, op0, op1), reads=reads, writes=writes)

    def rsq(out, a, scale, reads, writes):
        p.op("act", lambda e: e.activation(out, a, AF.Sqrt, bias=EPS, scale=scale), reads=reads, writes=writes)
        p.op("dve", lambda e: e.reciprocal(out, out), reads=writes, writes=writes)

    def stt(out, a, s, b, op0, op1, reads, writes):
        return p.op("dve", lambda e: e.scalar_tensor_tensor(out, a, s, b, op0, op1), reads=reads, writes=writes)

    def mm(out, lhsT, rhs, start, stop, reads, writes, skip=False):
        if skip:
            return p.op("pe", lambda e: e.matmul(out, lhsT, rhs, start=start, stop=stop, skip_group_check=True),
                        reads=reads, writes=writes)
        return p.op("pe", lambda e: e.matmul(out, lhsT, rhs, start=start, stop=stop), reads=reads, writes=writes)

    def pcol(l, c0, n=1):
        b = l * cfg.NP + c0
        return prm_sb[:, b:b + n]

    P_NMIX, P_NFFN, P_NPL = 0, KC, 2 * KC
    P_LNG, P_LNB = 3 * KC, 3 * KC + AG
    P_GQ, P_GK = 3 * KC + 2 * AG, 3 * KC + 2 * AG + 1

    dma("pool", cbf[:], cst[:, 0:NCB], [], [cbf_r])
    dma("sp", cff[:], cst[:, NCB:NCB + NCF], [], [cff_r])
    dma("sp", prm_sb[:].rearrange("p (l n) -> p l n", l=NL), prm.rearrange("l p n -> p l n"), [], [prm_r])

    hT_r = [Region(f"hT{k}") for k in range(KC)]
    dr = {nm: Region(nm) for nm in ("qT", "kT", "vO", "gaT", "gbT", "aoT")}

    jobs = []

    def XN(k):
        return XA.bf(k * 1024, 1024), XA.rg(k * 1024, 1024)

    def norm_job(l, hsrc, hsrc_regs, gain0, copy_to=None):
        def fn():
            NSTG = 8
            stg = Ring([(BC.f32(i * 2048, 1024), BC.rg(i * 2048, 2048)) for i in range(NSTG)])
            ssb = [(banks[0], bregs[0]), (banks[1], bregs[1])]
            for k in range(KC):
                st, sr = stg.next()
                dma("sp", st, hsrc[k * 128:(k + 1) * 128, :], [hsrc_regs[k]], sr)
                if copy_to is not None:
                    dma("sp", copy_to[k * 128:(k + 1) * 128, :], st, sr, [hT_r[k]])
                for h in range(2):
                    sq, sqr = tmpb.next()
                    actf(sq[:], st[:, h * 512:(h + 1) * 512], AF.Square, sr, [sqr])
                    mm(ssb[h][0][:], ones_b, sq[:], k == 0, k == KC - 1, [sqr, cbf_r], [ssb[h][1]])
            for h in range(2):
                rs = rstd[:, h * 512:(h + 1) * 512]
                rsq(rs, ssb[h][0][:], 1.0 / D, [ssb[h][1]], [rstd_r[h]])
            src = copy_to if copy_to is not None else hsrc
            for k in range(KC):
                st, sr = stg.next()
                rr = [hT_r[k]] if copy_to is not None else [hsrc_regs[k]]
                dma("sp", st, src[k * 128:(k + 1) * 128, :], rr, sr)
                xa, xr = XN(k)
                stt(xa, st, pcol(l, gain0 + k), rstd[:], ALU.mult, ALU.mult, sr + rstd_r + [prm_r], xr)
        jobs.append(Job("special", fn=fn, name="norm"))

    if mode in ("fused", "A"):
        cosT = MA.f32(8192, T)
        sinT = MA.f32(10240, T)
        cs_rl = MA.rg(8192, 4096)

        def rope_tables():
            posi = MA.f32(0, T).bitcast(I32)
            ang = MA.f32(2048, T)
            y = MA.f32(4096, T)
            fr = MA.f32(6144, T)
            r_all = MA.rg(0, 8192)
            dma("sp", posi, pos.partition_broadcast(128).rearrange("p o t -> p (o t)"), [], r_all)
            p.op("dve", lambda e: e.tensor_copy(ang, posi), reads=r_all, writes=r_all)
            ts(ang, ang, cff[:, CF_INVF:CF_INVF + 1], None, ALU.mult, None, r_all + [cff_r], r_all)
            C1 = 6.28125
            C2 = 2.0 * math.pi - 6.28125
            for which, dst in ((0, sinT), (1, cosT)):
                if which == 1:
                    ts(ang, ang, math.pi / 2, None, ALU.add, None, r_all, r_all)
                ts(y, ang, 1.0 / (2 * math.pi), None, ALU.mult, None, r_all, r_all)
                p.op("dve", lambda e: e.tensor_copy(posi, y), reads=r_all, writes=r_all)
                p.op("dve", lambda e: e.tensor_copy(y, posi), reads=r_all, writes=r_all)
                stt(fr, y, -C1, ang, ALU.mult, ALU.add, r_all, r_all)
                stt(fr, y, -C2, fr, ALU.mult, ALU.add, r_all, r_all)
                ts(fr, fr, 3.1415925, -3.1415925, ALU.min, ALU.max, r_all, r_all)
                actf(dst, fr, AF.Sin, r_all, cs_rl)

    def p1(l, hsrc, hsrc_regs, copy_to):
        jobs.append(Job("special", fn=rope_tables, name="rope"))
        norm_job(l, hsrc, hsrc_regs, P_NMIX, copy_to)
        w_in = W["w_in"][l]
        c_u, c_va, c_q, c_k, c_v = 0, AW, 2 * AW, 2 * AW + BW, 2 * AW + 2 * BW
        c_ga, c_gb = 2 * AW + 3 * BW, 2 * AW + 3 * BW + D

        def prep():
            ts(smallf[:, 0:1], pcol(l, P_GQ), 128.0 ** -0.5, None, ALU.mult, None, [prm_r], [small_r])
        jobs.append(Job("special", fn=prep))

        qk_stage = Ring([(BC.bf((8 + i) * 1024, 1024), BC.rg((8 + i) * 1024, 1024)) for i in range(4)])

        def qk_jobs(col0, gain_ap, gain_regs, dst, dst_r):
            for hd in range(BH):
                def epi(job, bk, hd=hd):
                    stg, stg_r = qk_stage.next()
                    sqs_ = []
                    for h in range(2):
                        pm, pmr = bk[h]
                        pa, par = bk[2 + h]
                        sq, sqr = tmpb.next()
                        actf(sq[:], pm[:], AF.Square, [pmr], [sqr])
                        mm(pa[:], ones_b, sq[:], True, True, [sqr, cbf_r], [par])
                    yield
                    qns = []
                    for h in range(2):
                        pm, pmr = bk[h]
                        pa, par = bk[2 + h]
                        r1, r1r = tmpf.next()
                        rsq(r1[:], pa[:], 1.0 / 128, [par], [r1r])
                        qn, qnr = tmpb.next()
                        stt(qn[:], pm[:], gain_ap, r1[:], ALU.mult, ALU.mult, [pmr, r1r] + gain_regs, [qnr])
                        mm(pa[:], rot_b, qn[:], True, True, [qnr, cbf_r], [par])
                        qns.append((qn, qnr))
                    for h in range(2):
                        pa, par = bk[2 + h]
                        qn, qnr = qns[h]
                        t1, t1r = tmpf.next()
                        tt(t1[:], qn[:], cosT[:, h * 512:(h + 1) * 512], ALU.mult, [qnr] + cs_rl, [t1r])
                        t2, t2r = tmpf.next()
                        tt(t2[:], pa[:], sinT[:, h * 512:(h + 1) * 512], ALU.mult, [par] + cs_rl, [t2r])
                        tt(stg[:, h * 512:(h + 1) * 512], t1[:], t2[:], ALU.add, [t1r, t2r], stg_r)
                    dma("sp", dst[hd * 128:(hd + 1) * 128, :], stg, stg_r, [dst_r])
                seg = Seg(w_in[:, col0 + hd * 128:col0 + (hd + 1) * 128], KC, 128, XN)
                jobs.append(Job("fm", [seg], epi, name="qk"))

        kdst = kown[l] if mode == "fused" else kT
        vdst = vown[l] if mode == "fused" else vO
        qk_jobs(c_k, pcol(l, P_GK), [prm_r], kdst, dr["kT"])
        qk_jobs(c_q, smallf[:, 0:1], [small_r], qT, dr["qT"])

        vstage = Ring([(BC.bf(i * 4096, 4096), BC.rg(i * 4096, 4096)) for i in range(2)])
        ncg_v = -(-BW // 512)
        for cg in range(ncg_v):
            ncols = min(512, BW - cg * 512)

            def epi_v(job, bk, cg=cg, ncols=ncols):
                stg, stg_r = vstage.next()
                sv = stg.rearrange("p (n e) -> p n e", e=512)
                for n in range(8):
                    if n % 2 == 0:
                        p.op("act", lambda e, n=n: e.copy(sv[:, n, 0:ncols], bk[n][0][:, 0:ncols]),
                             reads=[bk[n][1]], writes=stg_r)
                    else:
                        p.op("dve", lambda e, n=n: e.tensor_copy(sv[:, n, 0:ncols], bk[n][0][:, 0:ncols]),
                             reads=[bk[n][1]], writes=stg_r)
                dma("sp", vdst.rearrange("(n p) e -> p n e", p=128)[:, :, cg * 512:cg * 512 + ncols],
                    sv[:, :, 0:ncols], stg_r, [dr["vO"]])
            seg = Seg(w_in[:, c_v + cg * 512:c_v + cg * 512 + ncols], KC, ncols, XN)
            jobs.append(Job("tm", [seg], epi_v, name="v"))

        if mode == "fused":
            gather_job(l)

        ncg_a = -(-AW // 512)
        lnst = p.sbuf(f"lnst{l}", [128, 16 * ncg_a + 40], F32)
        lnst_r = Region("lnst")
        sums = lnst[:, 0:8 * ncg_a]
        sqs = lnst[:, 8 * ncg_a:16 * ncg_a]
        mean8 = lnst[:, 16 * ncg_a:16 * ncg_a + 8]
        rstd8 = lnst[:, 16 * ncg_a + 8:16 * ncg_a + 16]
        ex28 = lnst[:, 16 * ncg_a + 16:16 * ncg_a + 24]
        msq8 = lnst[:, 16 * ncg_a + 24:16 * ncg_a + 32]
        va = BC.bf(0, 8 * AW).rearrange("p (n e) -> p n e", e=AW)
        va_r = BC.rg(0, 8 * AW)
        for cg in range(ncg_a):
            ncols = min(512, AW - cg * 512)

            def epi_va(job, bk, cg=cg, ncols=ncols):
                for n in range(8):
                    ix = n * ncg_a + cg
                    p.op("act", lambda e, n=n, ix=ix: e.activation(va[:, n, cg * 512:cg * 512 + ncols], bk[n][0][:, 0:ncols],
                                                                   AF.Copy, accum_out=sums[:, ix:ix + 1]),
                         reads=[bk[n][1]], writes=va_r + [lnst_r])
                    jk, jkr = tmpb.next()
                    p.op("act", lambda e, n=n, ix=ix, jk=jk: e.activation(jk[:, 0:ncols], bk[n][0][:, 0:ncols],
                                                                          AF.Square, accum_out=sqs[:, ix:ix + 1]),
                         reads=[bk[n][1]], writes=[jkr, lnst_r])
            seg = Seg(w_in[:, c_va + cg * 512:c_va + cg * 512 + ncols], KC, ncols, XN)
            jobs.append(Job("tm", [seg], epi_va, name="va"))

        S_OFF = 16 * 1024

        def spatial():
            wsf = MA.f32(0, AG * 128).rearrange("p (g t) -> p g t", t=128)
            wsf_r = MA.rg(0, AG * 256)
            bsb = MA.f32(AG * 256, AG * 128).rearrange("p (g t) -> p g t", t=128)
            bsb_r = MA.rg(AG * 256, AG * 256)
            Bg = MA.f32(AG * 512, AG * 128).rearrange("p (g t) -> p g t", t=128)
            Bg_r = MA.rg(AG * 512, AG * 256)
            WTm = MA.bf(AG * 768, AG * 128).rearrange("p (g t) -> p g t", t=128)
            WTm_r = MA.rg(AG * 768, AG * 128)
            dma("sp", wsf, W["wsT"][l].rearrange("g s t -> s g t"), [], wsf_r)
            dma("sp", bsb.rearrange("p g t -> p (g t)"),
                W["bsp"][l].partition_broadcast(128).rearrange("p o n -> p (o n)"), [], bsb_r)
            for g in range(AG):
                tt(WTm[:, g, :], wsf[:, g, :], cff[:, CF_TRIL:CF_TRIL + 128], ALU.mult, wsf_r + [cff_r], WTm_r)
            for g in range(AG):
                b, br = banks[g // 4 % 2], bregs[g // 4 % 2]
                mm(b[:, (g % 4) * 128:(g % 4 + 1) * 128], ones_b, WTm[:, g, :], True, True, WTm_r + [cbf_r], [br])
                stt(Bg[:, g, :], b[:, (g % 4) * 128:(g % 4 + 1) * 128], pcol(l, P_LNB + g), bsb[:, g, :],
                    ALU.mult, ALU.add, [br, prm_r] + bsb_r, Bg_r)
            L = [lnst_r]
            p.op("dve", lambda e: e.tensor_reduce(mean8, sums.rearrange("p (n c) -> p n c", c=ncg_a), AX.X, ALU.add), reads=L, writes=L)
            p.op("dve", lambda e: e.tensor_reduce(ex28, sqs.rearrange("p (n c) -> p n c", c=ncg_a), AX.X, ALU.add), reads=L, writes=L)
            ts(mean8, mean8, 1.0 / AW, None, ALU.mult, None, L, L)
            ts(ex28, ex28, 1.0 / AW, None, ALU.mult, None, L, L)
            tt(msq8, mean8, mean8, ALU.mult, L, L)
            tt(ex28, ex28, msq8, ALU.subtract, L, L)
            rsq(rstd8, ex28, 1.0, L, L)
            for n in range(8):
                ts(va[:, n, :], va[:, n, :], mean8[:, n:n + 1], rstd8[:, n:n + 1],
                   ALU.subtract, ALU.mult, va_r + L, va_r)
            for g in range(AG):
                for half in range(2):
                    b, br = banks[2 + (2 * g + half) % 4], bregs[2 + (2 * g + half) % 4]
                    for nn in range(4):
                        n = half * 4 + nn
                        mm(b[:, nn * 128:(nn + 1) * 128], va[:, n, g * 128:(g + 1) * 128], WTm[:, g, :],
                           True, True, va_r + WTm_r, [br])
                    so = S_OFF + g * 1024 + half * 512
                    stt(BC.bf(so, 512).rearrange("p (n t) -> p n t", t=128),
                        b[:].rearrange("p (n t) -> p n t", t=128), pcol(l, P_LNG + g),
                        Bg[:, g, :].unsqueeze(1).to_broadcast([128, 4, 128]),
                        ALU.mult, ALU.add, [br, prm_r] + Bg_r, BC.rg(so, 512))
        jobs.append(Job("special", fn=spatial, name="spatial"))

        for m0 in range(0, AG, 2):
            nch = min(2, AG - m0)

            def epi_u(job, bk, m0=m0, nch=nch):
                for ci in range(nch):
                    for h in range(2):
                        so = S_OFF + (m0 + ci) * 1024 + h * 512
                        pm, pmr = bk[ci * 2 + h]
                        tt(BC.bf(so, 512), pm[:], BC.bf(so, 512), ALU.mult, [pmr] + BC.rg(so, 512), BC.rg(so, 512))
                if mode == "A":
                    dma("sp", aoT[m0 * 128:(m0 + nch) * 128, :].rearrange("(c p) t -> p c t", p=128),
                        BC.bf(S_OFF + m0 * 1024, nch * 1024).rearrange("p (c t) -> p c t", t=1024),
                        BC.rg(S_OFF + m0 * 1024, nch * 1024), [dr["aoT"]])
            seg = Seg(w_in[:, c_u + m0 * 128:c_u + (m0 + nch) * 128], KC, nch * 128, XN)
            jobs.append(Job("fm", [seg], epi_u, name="u"))

        gstage = Ring([(BC.bf(i * 2048, 2048), BC.rg(i * 2048, 2048)) for i in range(2)])
        for (c0, dst, dreg) in ((c_ga, gaT, dr["gaT"]), (c_gb, gbT, dr["gbT"])):
            for m0 in range(0, KC, 2):
                def epi_g(job, bk, m0=m0, dst=dst, dreg=dreg):
                    stg, stg_r = gstage.next()
                    for ci in range(2):
                        for h in range(2):
                            pm, pmr = bk[ci * 2 + h]
                            actf(stg[:, ci * 1024 + h * 512:ci * 1024 + (h + 1) * 512], pm[:], AF.Sigmoid, [pmr], stg_r)
                    dma("sp", dst[m0 * 128:(m0 + 2) * 128, :].rearrange("(c p) t -> p c t", p=128),
                        stg.rearrange("p (c t) -> p c t", t=1024), stg_r, [dreg])
                seg = Seg(w_in[:, c0 + m0 * 128:c0 + (m0 + 2) * 128], KC, 256, XN)
                jobs.append(Job("fm", [seg], epi_g, name="gate"))

    if mode == "fused":
        xidx_sb = p.sbuf("xidx_sb", [128, 48], I32)
        xidx_r = Region("xidx")
        dma("sp", xidx_sb[:], xidx, [], [xidx_r])
        XW = max(2 * T, BW)
        xch = Ring([(p.sbuf(f"xch{i}", [128, XW], BF16), Region(f"xch{i}")) for i in range(2)])
        kall_r = [Region(f"kall{l}") for l in layers]
        vall_r = [Region(f"vall{l}") for l in layers]
        kh_r = [Region("khalo")]
        vh_r = [Region("vhalo")]
        MK = 2
        nKb = BW // (128 * MK)
        nVb = T // 128
        assert nKb <= 8 and nVb <= 8

    def gather_job(l):
        def fn():
            p.op("pool", lambda e: e.collective_compute("AllGather", ALU.bypass, replica_groups=[list(range(NCORES))],
                                                       ins=[kown[l].opt()], outs=[kall[l].opt()]),
                 reads=[dr["kT"]], writes=[kall_r[l]], dma=True, cc=True)
            p.op("pool", lambda e: e.collective_compute("AllGather", ALU.bypass, replica_groups=[list(range(NCORES))],
                                                       ins=[vown[l].opt()], outs=[vall[l].opt()]),
                 reads=[dr["vO"]], writes=[vall_r[l]], dma=True, cc=True)
        jobs.append(Job("special", fn=fn, name="allgather"))

    def halo_job(l):
        def fn():
            kv = kall[l].rearrange("(n m) t -> n m t", m=MK)
            for j in range(3):
                for b in range(nKb):
                    buf, br = xch.next()
                    col = j * 8 + b
                    dst = buf[:, 0:MK * T].rearrange("p (m t) -> p m t", m=MK)
                    p.op("pool", lambda e, dst=dst, col=col: e.indirect_dma_start(
                        out=dst, out_offset=None, in_=kv,
                        in_offset=bass.IndirectOffsetOnAxis(xidx_sb[:, col:col + 1], 0)),
                        reads=[kall_r[l], xidx_r], writes=[br], dma=True)
                    dma("sp", khalo[j].rearrange("(b p m) t -> b p m t", p=128, m=MK)[b], dst, [br], kh_r)
            for j in range(3):
                for b in range(nVb):
                    buf, br = xch.next()
                    col = 24 + j * 8 + b
                    dst = buf[:, 0:BW]
                    p.op("pool", lambda e, dst=dst, col=col: e.indirect_dma_start(
                        out=dst, out_offset=None, in_=vall[l],
                        in_offset=bass.IndirectOffsetOnAxis(xidx_sb[:, col:col + 1], 0)),
                        reads=[vall_r[l], xidx_r], writes=[br], dma=True)
                    dma("sp", vhalo[j * T + b * 128:j * T + (b + 1) * 128, :], dst, [br], vh_r)
        jobs.append(Job("special", fn=fn, name="halo"))

    def p2(l, ksrc, ksrc_r, vsrc, vsrc_r):
        negc0 = smallf[:, 2:3]

        def shift():
            rows = MA.f32(0, 256, parts=1)
            rows_r = MA.rg(0, 512)
            dma("sp", rows.rearrange("p (a d) -> p a d", a=2), nrow[l:l + 1], [], rows_r)
            mx = MA.f32(512, 2, parts=1)
            p.op("dve", lambda e: e.tensor_reduce(mx, rows.rearrange("p (a d) -> p a d", a=2), AX.X, ALU.max,
                                                  apply_absolute_value=True), reads=rows_r, writes=rows_r)
            c0b = MA.bf(1024, 1, parts=1)
            c0r = MA.rg(1024, 1)
            stt(c0b, mx[:, 0:1], -math.sqrt(128.0), mx[:, 1:2], ALU.mult, ALU.mult, rows_r, c0r)
            mm(banks[7][:, 0:1], cbf[0:1, CB_ONES:CB_ONES + 128], c0b, True, True, c0r + [cbf_r], [bregs[7]])
            p.op("dve", lambda e: e.tensor_copy(negc0, banks[7][:, 0:1]), reads=[bregs[7]], writes=[small_r])
        jobs.append(Job("special", fn=shift, name="shift"))

        KB = [0, 6144]
        QB = [12288, 14336]
        V1, V4, V16A, V16B = 16384, 16384 + 2304, 16384 + 2304 + 3072, 16384 + 2304 + 3072 + 4096
        ET = Ring([(et_tiles[i][:, :], [Region(f"et{i}")]) for i in range(4)])
        Ob = [(banks[0], bregs[0]), (banks[1], bregs[1])]
        Db = [(banks[2], bregs[2]), (banks[3], bregs[3])]
        Sb = Ring([(banks[4], bregs[4]), (banks[5], bregs[5]), (banks[6], bregs[6])])
        NHP = BH // 2

        def load_kq(hp):
            kb = XA.bf(KB[hp % 2], 6144).rearrange("p (h j t) -> p h j t", h=2, j=3)
            kr = XA.rg(KB[hp % 2], 6144)
            for j in range(3):
                dma("sp", kb[:, :, j, :], ksrc(j)[hp * 256:(hp + 1) * 256, :].rearrange("(h d) t -> d h t", d=128),
                    ksrc_r, kr)
            qb = XA.bf(QB[hp % 2], 2048).rearrange("p (h t) -> p h t", h=2)
            dma("sp", qb, qT[hp * 256:(hp + 1) * 256, :].rearrange("(h d) t -> d h t", d=128),
                [dr["qT"]], XA.rg(QB[hp % 2], 2048))

        def load_v(hp, which):
            cs = slice(hp * 256, (hp + 1) * 256)
            if which == 1:
                dma("sp", XA.bf(V1, 2304).rearrange("p (j e) -> p j e", e=256),
                    vsrc[1920:1920 + 1152, cs].rearrange("(j i) e -> i j e", i=128), vsrc_r, XA.rg(V1, 2304))
            elif which == 4:
                v4 = XA.bf(V4, 3072).rearrange("p (j r e) -> p j r e", j=3, r=4)
                for j in range(3):
                    dma("sp", v4[:, j, :, :],
                        vsrc[1536 + 512 * j:1536 + 512 * (j + 1), cs].rearrange("(i r) e -> i r e", r=4),
                        vsrc_r, XA.rg(V4, 3072))
            else:
                dma("sp", XA.bf(V16A, 4096).rearrange("p (r e) -> p r e", e=256),
                    vsrc[0:2048, cs].rearrange("(i r) e -> i r e", r=16), vsrc_r, XA.rg(V16A, 4096))
                dma("sp", XA.bf(V16B, 4096, parts=64).rearrange("p (r e) -> p r e", e=256),
                    vsrc[2048:3072, cs].rearrange("(i r) e -> i r e", r=16), vsrc_r, XA.rg(V16B, 4096))

        def head(hp, hh):
            h = hp * 2 + hh
            kflat = XA.bf(KB[hp % 2] + hh * 3072, 3072)
            k_r = XA.rg(KB[hp % 2], 6144)
            qf = XA.bf(QB[hp % 2] + hh * 1024, 1024)
            q_r = XA.rg(QB[hp % 2], 2048)
            v1 = XA.bf(V1, 2304).rearrange("p (j e) -> p j e", e=256)
            v4 = XA.bf(V4, 3072).rearrange("p (j r e) -> p j r e", j=3, r=4)
            v16a = XA.bf(V16A, 4096).rearrange("p (r e) -> p r e", e=256)
            v16b = XA.bf(V16B, 4096, parts=64).rearrange("p (r e) -> p r e", e=256)
            es = slice(hh * 128, (hh + 1) * 128)
            for (b, br) in Ob + Db:
                p.op("dve", lambda e, b=b: e.memset(b[:], 0.0), writes=[br])
            tiles = []
            for j in range(9):
                kap = kflat[:, 1920 + 128 * j:1920 + 128 * (j + 1)]
                if j == 0:
                    qap, nq, mask = qf[:, 0:128], 128, cbf[:, CB_MPH1:CB_MPH1 + 128]
                    outs = [(0, slice(0, 128), slice(0, 128))]
                elif j == 8:
                    qap, nq, mask = qf[:, 896:1024], 128, cbf[:, CB_MOMP:CB_MOMP + 128]
                    outs = [(1, slice(384, 512), slice(0, 128))]
                else:
                    qap, nq, mask = qf[:, 128 * (j - 1):128 * (j + 1)], 256, cbf[:, CB_MOMP:CB_MOMP + 256]
                    outs = []
                    for t in range(2):
                        blk = j - 1 + t
                        outs.append((blk // 4, slice((blk % 4) * 128, (blk % 4 + 1) * 128), slice(t * 128, (t + 1) * 128)))
                tiles.append((128, kap, qap, nq, mask, v1[:, j, es], XA.rg(V1, 2304), outs))
            for r in range(4):
                for j in range(3):
                    kap = kflat[:, 1536 + 512 * j + r:1536 + 512 * (j + 1):4]
                    if j == 0:
                        qap, nq, mask = qf[:, r:512:4], 128, cbf[:, CB_MPH4:CB_MPH4 + 128]
                        outs = [(0, slice(r, 512, 4), slice(0, 128))]
                    elif j == 2:
                        qap, nq, mask = qf[:, 512 + r:1024:4], 128, cbf[:, CB_MOMP:CB_MOMP + 128]
                        outs = [(1, slice(r, 512, 4), slice(0, 128))]
                    else:
                        qap, nq, mask = qf[:, r:1024:4], 256, cbf[:, CB_MOMP:CB_MOMP + 256]
                        outs = [(0, slice(r, 512, 4), slice(0, 128)), (1, slice(r, 512, 4), slice(128, 256))]
                    tiles.append((128, kap, qap, nq, mask, v4[:, j, r, es], XA.rg(V4, 3072), outs))
            for r in range(16):
                qap = qf[:, r:1024:16]
                outs = [(0, slice(r, 512, 16), slice(0, 32)), (1, slice(r, 512, 16), slice(32, 64))]
                tiles.append((128, kflat[:, r:2048:16], qap, 64, cbf[:, CB_MA16:CB_MA16 + 64],
                              v16a[:, r, es], XA.rg(V16A, 4096), outs))
                tiles.append((64, kflat[:, 2048 + r:3072:16], qap, 64, cbf[0:64, CB_MOMP:CB_MOMP + 64],
                              v16b[:, r, es], XA.rg(V16B, 4096), outs))

            groups = []
            cur, off = [], 0
            for tl in tiles:
                nq = tl[3]
                if off + nq > 512:
                    groups.append(cur)
                    cur, off = [], 0
                cur.append((tl, off))
                off += nq
            groups.append(cur)

            def s_stage(grp):
                assert grp[0][0][0] == 128
                sb, sbr = Sb.next()
                used = 0
                for gi, (tl, off) in enumerate(grp):
                    nk, kap, qap, nq, mask, vap, vr, outs = tl
                    mm(sb[0:nk, off:off + nq], kap, qap, gi == 0, False, k_r + q_r, [sbr], skip=(gi > 0))
                    mm(sb[0:nk, off:off + nq], cbf[0:nk, CB_ID:CB_ID + nk], mask, False, True, [cbf_r], [sbr], skip=True)
                    used = off + nq
                et, etr = ET.next()
                actf(et[:, 0:used], sb[:, 0:used], AF.Exp, [sbr, small_r], etr, bias=negc0)
                return et, etr

            def pv_stage(grp, et, etr):
                for (tl, off) in grp:
                    nk, kap, qap, nq, mask, vap, vr, outs = tl
                    for (bi, oc, ec) in outs:
                        e0, e1 = off + ec.start, off + ec.stop
                        mm(Ob[bi][0][:, oc], vap, et[0:nk, e0:e1], False, False, vr + etr, [Ob[bi][1]], skip=True)
                        mm(Db[bi][0][:, oc], cbf[0:nk, CB_ONES:CB_ONES + 128], et[0:nk, e0:e1], False, False,
                           etr + [cbf_r], [Db[bi][1]], skip=True)

            pend = []
            for grp in groups:
                pend.append((grp,) + s_stage(grp))
                if len(pend) > 2:
                    pv_stage(*pend.pop(0))
            while pend:
                pv_stage(*pend.pop(0))
            for half in range(2):
                rd, rdr = tmpf.next()
                p.op("dve", lambda e, rd=rd, half=half: e.reciprocal(rd[:], Db[half][0][:]), reads=[Db[half][1]], writes=[rdr])
                tt(BC.bf(h * 1024 + half * 512, 512), Ob[half][0][:], rd[:], ALU.mult, [Ob[half][1], rdr],
                   BC.rg(h * 1024 + half * 512, 512))

        def attn():
            load_kq(0)
            for w in (1, 4, 16):
                load_v(0, w)
            for hp in range(NHP):
                if hp + 1 < NHP:
                    load_kq(hp + 1)
                head(hp, 0)
                head(hp, 1)
                if hp + 1 < NHP:
                    for w in (1, 4, 16):
                        load_v(hp + 1, w)
        jobs.append(Job("special", fn=attn, name="attn"))

    ostage = Ring([(MA.f32(8192 + i * 4096, 2048), MA.rg(8192 + i * 4096, 4096)) for i in range(2)])

    def acc_epi(m0, nch):
        def epi(job, bk):
            stg, stg_r = ostage.next()
            for ci in range(nch):
                for h in range(2):
                    pm, pmr = bk[ci * 2 + h]
                    dst = stg[:, ci * 1024 + h * 512:ci * 1024 + (h + 1) * 512]
                    if h == 0:
                        p.op("act", lambda e, dst=dst, pm=pm: e.copy(dst, pm[:]), reads=[pmr], writes=stg_r)
                    else:
                        p.op("dve", lambda e, dst=dst, pm=pm: e.tensor_copy(dst, pm[:]), reads=[pmr], writes=stg_r)
            dma("pool", hT[m0 * 128:(m0 + nch) * 128, :].rearrange("(c p) t -> p c t", p=128),
                stg[:, 0:nch * 1024].rearrange("p (c t) -> p c t", t=1024), stg_r,
                [hT_r[m0 + c] for c in range(nch)], accum_op=ALU.add)
        return epi

    def p3(l):
        def AO(k):
            return BC.bf(16 * 1024 + k * 1024, 1024), BC.rg(16 * 1024 + k * 1024, 1024)

        def BO(k):
            return BC.bf(k * 1024, 1024), BC.rg(k * 1024, 1024)

        gt = Ring([(MA.bf(i * 2048, 2048), MA.rg(i * 2048, 2048)) for i in range(2)])
        for m in range(KC):
            st8 = {}

            def pre(job, m=m, st8=st8):
                g, gr = gt.next()
                dma("sp", g[:, 0:1024], gaT[m * 128:(m + 1) * 128, :], [dr["gaT"]], gr)
                dma("sp", g[:, 1024:2048], gbT[m * 128:(m + 1) * 128, :], [dr["gbT"]], gr)
                st8["g"] = (g, gr)

            def epi(job, bk, m=m, st8=st8):
                g, gr = st8["g"]
                xa, xr = XN(m)
                for h in range(2):
                    t1, t1r = tmpf.next()
                    tt(t1[:], bk[h][0][:], g[:, h * 512:(h + 1) * 512], ALU.mult, [bk[h][1]] + gr, [t1r])
                    t2, t2r = tmpf.next()
                    tt(t2[:], bk[2 + h][0][:], g[:, 1024 + h * 512:1024 + (h + 1) * 512], ALU.mult, [bk[2 + h][1]] + gr, [t2r])
                    tt(xa[:, h * 512:(h + 1) * 512], t1[:], t2[:], ALU.add, [t1r, t2r], xr)
            segs = [Seg(W["w_br_a"][l][:, m * 128:(m + 1) * 128], AG, 128, AO),
                    Seg(W["w_br_b"][l][:, m * 128:(m + 1) * 128], BH, 128, BO)]
            jobs.append(Job("fm", segs, epi, pre=pre, name="merge"))
        for m0 in range(0, KC, 2):
            jobs.append(Job("fm", [Seg(W["w_out"][l][:, m0 * 128:(m0 + 2) * 128], KC, 256, XN)], acc_epi(m0, 2), name="wout"))
        norm_job(l, hT, hT_r, P_NFFN)
        for (f0, f1) in cfg.ffn_blocks:
            def ACT_(k):
                return BC.bf(k * 1024, 1024), BC.rg(k * 1024, 1024)
            for f in range(f0, f1):
                def epi(job, bk, f=f, f0=f0):
                    for h in range(2):
                        sg, sgr = tmpf.next()
                        actf(sg[:], bk[h][0][:], AF.Silu, [bk[h][1]], [sgr])
                        so = (f - f0) * 1024 + h * 512
                        tt(BC.bf(so, 512), bk[2 + h][0][:], sg[:], ALU.mult, [bk[2 + h][1], sgr], BC.rg(so, 512))
                segs = [Seg(W["w_ffn_gate"][l][:, f * 128:(f + 1) * 128], KC, 128, XN),
                        Seg(W["w_ffn_up"][l][:, f * 128:(f + 1) * 128], KC, 128, XN)]
                jobs.append(Job("fm", segs, epi, name="ffn_gu"))
            for m0 in range(0, KC, 2):
                jobs.append(Job("fm", [Seg(W["w_ffn_down"][l][f0 * 128:f1 * 128, m0 * 128:(m0 + 2) * 128], f1 - f0, 256, ACT_)],
                                acc_epi(m0, 2), name="ffn_down"))
        norm_job(l, hT, hT_r, P_NPL)
        pTb = MA.bf(4096, (PL // 128) * 1024)
        pTb_r = MA.rg(4096, (PL // 128) * 1024)

        def loadp():
            dma("pool", pTb.rearrange("p (c t) -> p c t", t=1024), pT[l].rearrange("(c p) t -> p c t", p=128), [], pTb_r)
        jobs.append(Job("special", fn=loadp))

        def PT(k):
            return pTb[:, k * 1024:(k + 1) * 1024], pTb_r
        for m in range(KC):
            def epi(job, bk, m=m):
                stg, stg_r = ostage.next()
                for h in range(2):
                    sg, sgr = tmpf.next()
                    actf(sg[:], bk[h][0][:], AF.Sigmoid, [bk[h][1]], [sgr])
                    tt(stg[:, h * 512:(h + 1) * 512], bk[2 + h][0][:], sg[:], ALU.mult, [bk[2 + h][1], sgr], stg_r)
                dma("pool", hT[m * 128:(m + 1) * 128, :], stg[:, 0:1024], stg_r, [hT_r[m]], accum_op=ALU.add)
            segs = [Seg(W["w_pl_gate"][l][:, m * 128:(m + 1) * 128], KC, 128, XN),
                    Seg(W["w_pl_proj"][l][:, m * 128:(m + 1) * 128], PL // 128, 128, PT)]
            jobs.append(Job("fm", segs, epi, name="pl"))

    hin_r = [Region(f"hin{k}") for k in range(KC)]
    if mode == "fused":
        for l in layers:
            if l == 0:
                p1(l, xT, hin_r, hT)
            else:
                p1(l, hT, hT_r, None)
            halo_job(l)
            p2(l, lambda j: khalo[j], kh_r, vhalo, vh_r)
            p3(l)
    elif mode == "A":
        p1(0, hin, hin_r, None)
    elif mode == "B":
        def init():
            stg = Ring([(BC.f32(i * 2048, 1024), BC.rg(i * 2048, 2048)) for i in range(8)])
            for k in range(KC):
                st, sr = stg.next()
                dma("sp", st, hin[k * 128:(k + 1) * 128, :], [hin_r[k]], sr)
                dma("sp", hT[k * 128:(k + 1) * 128, :], st, sr, [hT_r[k]])
        jobs.append(Job("special", fn=init))
        kh_r = [Region("khalo")]
        vh_r = [Region("vhalo")]
        p2(0, lambda j: khalo[j], kh_r, vhalo, vh_r)

        def load_ao():
            dma("sp", BC.bf(16 * 1024, AG * 1024).rearrange("p (c t) -> p c t", t=1024),
                aoT.rearrange("(c p) t -> p c t", p=128), [dr["aoT"]], BC.rg(16 * 1024, AG * 1024))
        jobs.append(Job("special", fn=load_ao))
        p3(0)

    import os as _os
    _ks = int(_os.environ.get("KSTOP", "0"))
    if _ks:
        jobs = jobs[:_ks]
    entries = []
    njob = 0
    for jb in jobs:
        if jb.kind == "special":
            entries.append(("special", jb))
            continue
        if jb.kind == "fm":
            jb.set = njob % 2
            njob += 1
        first = True
        for si, sg in enumerate(jb.segs):
            ks = 2048 // (sg.ncols if jb.kind == "fm" else 512)
            k0 = 0
            while k0 < sg.nk:
                nk = min(ks, sg.nk - k0)
                entries.append(("slab", jb, si, k0, nk, first))
                first = False
                k0 += nk
        entries.append(("epi", jb))

    slab_ids = [i for i, e in enumerate(entries) if e[0] == "slab"]
    slab_slot = {}
    for n, i in enumerate(slab_ids):
        slab_slot[i] = n % NS
    LA = NS - 1
    state = {"next": 0}

    def slot_view(i):
        _, jb, si, k0, nk, _ = entries[i]
        sg = jb.segs[si]
        return wslots[slab_slot[i]][:, 0:nk * sg.ncols].rearrange("p (k m) -> p k m", m=sg.ncols)

    def issue_loads(upto):
        while state["next"] < min(upto, len(slab_ids)):
            i = slab_ids[state["next"]]
            _, jb, si, k0, nk, _ = entries[i]
            sg = jb.segs[si]
            dma("pool", slot_view(i), sg.W[k0 * 128:(k0 + nk) * 128, :].rearrange("(k p) m -> p k m", p=128),
                [], [wregs[slab_slot[i]]])
            state["next"] += 1

    ordinal = {i: n for n, i in enumerate(slab_ids)}
    pending = []

    def step_pending(drain=False, only_set=None):
        while True:
            did = False
            for ent_ in list(pending):
                g, gs = ent_
                if only_set is not None and gs != only_set:
                    continue
                did = True
                try:
                    next(g)
                except StopIteration:
                    pending.remove(ent_)
            if not drain or not did:
                break

    for i, ent in enumerate(entries):
        if ent[0] == "special":
            step_pending(drain=True)
            nxt = [n for n, j in enumerate(slab_ids) if j > i]
            if nxt:
                issue_loads(nxt[0] + LA)
            ent[1].fn()
        elif ent[0] == "slab":
            _, jb, si, k0, nk, first = ent
            if first:
                if jb.kind == "tm":
                    step_pending(drain=True)
                else:
                    step_pending(drain=True, only_set=jb.set)
            issue_loads(ordinal[i] + LA + 1)
            if first and jb.pre is not None:
                jb.pre(jb)
            sg = jb.segs[si]
            sv = slot_view(i)
            wr = wregs[slab_slot[i]]
            if jb.kind == "fm":
                chunk0 = sum(s.ncols // 128 for s in jb.segs[:si])
                for kk in range(nk):
                    k = k0 + kk
                    xap, xr = sg.xin(k)
                    for ci in range(sg.ncols // 128):
                        for h in range(2):
                            b = jb.set * 4 + (chunk0 + ci) * 2 + h
                            mm(banks[b][:], sv[:, kk, ci * 128:(ci + 1) * 128], xap[:, h * 512:(h + 1) * 512],
                               k == 0, k == sg.nk - 1, [wr] + xr, [bregs[b]])
            else:
                for kk in range(nk):
                    k = k0 + kk
                    xap, xr = sg.xin(k)
                    for n in range(8):
                        mm(banks[n][:, 0:sg.ncols], xap[:, n * 128:(n + 1) * 128], sv[:, kk, :],
                           k == 0, k == sg.nk - 1, [wr] + xr, [bregs[n]])
            step_pending()
        else:
            jb = ent[1]
            if jb.kind == "fm":
                bk = [(banks[jb.set * 4 + j], bregs[jb.set * 4 + j]) for j in range(4)]
            else:
                bk = [(banks[j], bregs[j]) for j in range(8)]
                step_pending(drain=True)
            r_ = jb.epi(jb, bk)
            if r_ is not None and hasattr(r_, "__next__"):
                pending.append((r_, getattr(jb, "set", -1)))
    step_pending(drain=True)

    if mode == "A":
        outs = list(dr.values())
    else:
        outs = hT_r
    p.op("sp", lambda e: e.nop(), reads=outs)
    p.emit()
    return nc, p


def make_consts(core):
    cb = np.zeros((128, NCB), np.float32)
    i = np.arange(128)[:, None]
    j = np.arange(128)[None, :]
    cb[:, CB_ID:CB_ID + 128] = np.eye(128)
    cb[:, CB_ONES:CB_ONES + 128] = 1.0
    rot = np.zeros((128, 128), np.float32)
    for d in range(16):
        rot[d + 16, d] = -1.0
        rot[d, d + 16] = 1.0
    cb[:, CB_ROT:CB_ROT + 128] = rot
    MO = np.where(i <= j, 0.0, NEG)
    MP = np.where(i >= j, 0.0, NEG)
    cb[:, CB_MOMP:CB_MOMP + 128] = MO
    cb[:, CB_MOMP + 128:CB_MOMP + 256] = MP
    allneg = np.full((128, 128), NEG)
    cb[:, CB_MPH1:CB_MPH1 + 128] = allneg if core == 0 else MP
    cb[:, CB_MPH4:CB_MPH4 + 128] = allneg if core == 0 else MP
    ma = MP[:, :64].copy()
    if core == 0:
        ma[:] = NEG
    elif core == 1:
        ma[:64, :] = NEG
    cb[:, CB_MA16:CB_MA16 + 64] = ma
    cf = np.zeros((128, NCF), np.float32)
    cf[:, CF_TRIL:CF_TRIL + 128] = (i <= j)
    half = 16
    invf = np.power(np.float32(500000.0), -np.arange(half, dtype=np.float32) / np.float32(half)).astype(np.float32)
    col = np.zeros(128, np.float32)
    col[:16] = invf
    col[16:32] = invf
    cf[:, CF_INVF] = col
    return np.concatenate([cb, cf], axis=1)


def col_layout(v):
    return np.ascontiguousarray(v.reshape(-1, 128).T)


def make_prm(cfg, inp, l):
    parts = [col_layout(inp["norm_mix"][l]), col_layout(inp["norm_ffn"][l]), col_layout(inp["norm_pl"][l]),
             col_layout(inp["v_ln_g"][l]), col_layout(inp["v_ln_b"][l]),
             inp["q_norm"][l].reshape(128, 1), inp["k_norm"][l].reshape(128, 1)]
    return np.ascontiguousarray(np.concatenate(parts, axis=1).astype(np.float32))


_CACHE = {}


def get_prog(cfg, mode):
    key = (cfg.D, cfg.AG, cfg.BH, cfg.FF, mode)
    if key not in _CACHE:
        _CACHE[key] = build(cfg, mode)[0]
    return _CACHE[key]


def run_unfused(cfg, inp):
    inp = {k: np.asarray(v) for k, v in inp.items()}
    S = NCORES * T
    x = inp["x"].reshape(S, cfg.D)
    hT = [np.ascontiguousarray(x[c * T:(c + 1) * T].T) for c in range(NCORES)]
    consts = [make_consts(c) for c in range(NCORES)]
    pos = inp["positions"].reshape(S).astype(np.int32)
    ncA = get_prog(cfg, "A")
    ncB = get_prog(cfg, "B")
    bf = ml_dtypes.bfloat16
    for l in range(cfg.DEPTH):
        prm = make_prm(cfg, inp, l)[None]
        wsT = np.ascontiguousarray(inp["w_spatial"][l].transpose(0, 2, 1))[None]
        common_a = {
            "prm": prm, "w_in": inp["w_in"][l][None], "wsT": wsT,
            "bsp": inp["b_spatial"][l].reshape(1, 1, -1),
        }
        maps = [dict(common_a, cst=consts[c], hin=hT[c], pos=pos[c * T:(c + 1) * T].reshape(1, T)) for c in range(NCORES)]
        ra = run_bass_kernel_spmd(ncA, maps, core_ids=list(range(NCORES))).results
        zk = np.zeros((cfg.BW, T), bf)
        zv = np.zeros((T, cfg.BW), bf)
        common_b = {"prm": prm, "nrow": np.stack([inp["q_norm"][l], inp["k_norm"][l]])[None]}
        for nm in ("w_br_a", "w_br_b", "w_out", "w_ffn_gate", "w_ffn_up", "w_ffn_down", "w_pl_gate", "w_pl_proj"):
            common_b[nm] = inp[nm][l][None]
        maps = []
        for c in range(NCORES):
            kh = np.stack([ra[c - 2 + j]["kT"] if c - 2 + j >= 0 else zk for j in range(3)])
            vh = np.concatenate([ra[c - 2 + j]["vO"] if c - 2 + j >= 0 else zv for j in range(3)], axis=0)
            pTl = np.ascontiguousarray(inp["p"][l].reshape(S, cfg.PL)[c * T:(c + 1) * T].T)[None]
            maps.append(dict(common_b, cst=consts[c], hin=hT[c], qT=ra[c]["qT"], khalo=kh, vhalo=vh,
                             gaT=ra[c]["gaT"], gbT=ra[c]["gbT"], aoT=ra[c]["aoT"], pT=pTl))
        rb = run_bass_kernel_spmd(ncB, maps, core_ids=list(range(NCORES))).results
        hT = [rb[c]["hT"] for c in range(NCORES)]
    out = np.concatenate([h.T for h in hT], axis=0).reshape(1, S, cfg.D)
    return np.ascontiguousarray(out.astype(np.float32))


def make_xidx(cfg, core):
    idx = np.zeros((128, 48), np.int32)
    pp = np.arange(128)
    MK = 2
    for j in range(3):
        r = max(core - 2 + j, 0)
        for b in range(8):
            idx[:, j * 8 + b] = r * (cfg.BW // MK) + b * 128 + pp
            idx[:, 24 + j * 8 + b] = r * T + b * 128 + pp
    return idx


def run_fused(cfg, inp):
    inp = {k: np.asarray(v) for k, v in inp.items()}
    S = NCORES * T
    x = inp["x"].reshape(S, cfg.D)
    pos = inp["positions"].reshape(S).astype(np.int32)
    nc = get_prog(cfg, "fused")
    L = cfg.DEPTH
    common = {
        "prm": np.stack([make_prm(cfg, inp, l) for l in range(L)]),
        "w_in": inp["w_in"],
        "wsT": np.ascontiguousarray(inp["w_spatial"].transpose(0, 1, 3, 2)),
        "bsp": np.ascontiguousarray(inp["b_spatial"].reshape(L, 1, -1)),
        "nrow": np.ascontiguousarray(np.stack([inp["q_norm"], inp["k_norm"]], axis=1)),
    }
    for nm in ("w_br_a", "w_br_b", "w_out", "w_ffn_gate", "w_ffn_up", "w_ffn_down", "w_pl_gate", "w_pl_proj"):
        common[nm] = inp[nm]
    pall = inp["p"].reshape(L, S, cfg.PL)
    maps = []
    for c in range(NCORES):
        m = dict(common)
        m["cst"] = make_consts(c)
        m["xidx"] = make_xidx(cfg, c)
        m["pos"] = pos[c * T:(c + 1) * T].reshape(1, T)
        m["xT"] = np.ascontiguousarray(x[c * T:(c + 1) * T].T)
        m["pT"] = np.ascontiguousarray(pall[:, c * T:(c + 1) * T, :].transpose(0, 2, 1))
        maps.append(m)
    res = run_bass_kernel_spmd(nc, maps, core_ids=list(range(NCORES))).results
    out = np.concatenate([res[c]["hT"].T for c in range(NCORES)], axis=0).reshape(1, S, cfg.D)
    return np.ascontiguousarray(out.astype(np.float32))


def kernel(**inputs):
    return run_unfused(FULL, inputs)
```
